# Optimizing a Trainium2 kernel written in Bass

```python
import jax, jax.numpy as jnp
from jax import lax
import numpy as np

D_MODEL = 2048
BATCH = 16
SEQ = 2048
DEPTH = 1
DEC_BATCH = 16
DEC_SEQ = 64
PAST_LEN = 2048

CHUNK = 64
Q_BLOCK = 128
N_HEADS_A = 8
HEAD_DIM_A = 128
ATTN_WIDTH = N_HEADS_A * HEAD_DIM_A
POOL_WINDOWS = (2, 4, 8, 16)
N_POOL_GROUPS = len(POOL_WINDOWS)
POOL_WIDTH = D_MODEL // 2
POOL_GROUP = POOL_WIDTH // N_POOL_GROUPS
POOL_OUT_GROUP = D_MODEL // N_POOL_GROUPS
POOL_HIST = max(POOL_WINDOWS) - 1
D_FF = 5632
RMS_EPS = 1e-6
NEG_INF = -1e30
IN_WIDTH = 3 * ATTN_WIDTH + N_HEADS_A + POOL_WIDTH + 2 * D_MODEL
IN_SPLITS = (ATTN_WIDTH, 2 * ATTN_WIDTH, 3 * ATTN_WIDTH, 3 * ATTN_WIDTH + N_HEADS_A,
             3 * ATTN_WIDTH + N_HEADS_A + POOL_WIDTH, 3 * ATTN_WIDTH + N_HEADS_A + POOL_WIDTH + D_MODEL)

kernel_name = "fox_pool_macaron_streaming_step"


def _rmsnorm(x, g):
    xf = x.astype(jnp.float32)
    r = lax.rsqrt(jnp.mean(xf * xf, axis=-1, keepdims=True) + RMS_EPS)
    return (xf * r).astype(x.dtype) * g


def _half_ffn(x, g, w_gate, w_up, w_down):
    h = _rmsnorm(x, g)
    return (jax.nn.silu(h @ w_gate) * (h @ w_up)) @ w_down


def _fox_block(q, cq, pq, k, v, ck, pk):
    s = jnp.einsum('bqhd,bkhd->bhqk', q, k).astype(jnp.float32) * (HEAD_DIM_A ** -0.5)
    bias = jnp.swapaxes(cq, 1, 2)[:, :, :, None] - jnp.swapaxes(ck, 1, 2)[:, :, None, :]
    mask = pk[None, :] <= pq[:, None]
    p = jax.nn.softmax(jnp.where(mask, s + bias, NEG_INF), axis=-1)
    return jnp.einsum('bhqk,bkhd->bqhd', p.astype(v.dtype), v)


def _fox_prompt(q, k, v, c):
    B, S, H, Dh = q.shape
    nb = S // Q_BLOCK
    pos = jnp.arange(S)
    qb = jnp.swapaxes(q.reshape(B, nb, Q_BLOCK, H, Dh), 0, 1)
    cb = jnp.swapaxes(c.reshape(B, nb, Q_BLOCK, H), 0, 1)
    pb = pos.reshape(nb, Q_BLOCK)
    ob = lax.map(lambda a: _fox_block(a[0], a[1], a[2], k, v, c, pos), (qb, cb, pb))
    return jnp.swapaxes(ob, 0, 1).reshape(B, S, H * Dh)


def _fox_sample(q, k, v, logf, k_past, v_past, logf_past):
    B, T, H, Dh = q.shape
    P = k_past.shape[1]
    k_all = jnp.concatenate([k_past, k], axis=1)
    v_all = jnp.concatenate([v_past, v], axis=1)
    c_all = jnp.cumsum(jnp.concatenate([logf_past.astype(jnp.float32), logf], axis=1), axis=1)
    pos_all = jnp.arange(P + T)
    o = _fox_block(q, c_all[:, P:], P + jnp.arange(T), k_all, v_all, c_all, pos_all)
    return o.reshape(B, T, H * Dh)


def _pool_branch(u, hist, pos0, w_pool_group, pool_scale):
    B, T, C = u.shape
    padded = jnp.concatenate([hist, u], axis=1).astype(jnp.float32)
    cs = jnp.concatenate([jnp.zeros((B, 1, C), jnp.float32), jnp.cumsum(padded, axis=1)], axis=1)
    pos = pos0 + jnp.arange(T)
    uf = u.astype(jnp.float32)
    outs = []
    for g, w in enumerate(POOL_WINDOWS):
        sl = slice(g * POOL_GROUP, (g + 1) * POOL_GROUP)
        csg = cs[:, :, sl]
        win_sum = csg[:, POOL_HIST + 1:POOL_HIST + 1 + T] - csg[:, POOL_HIST + 1 - w:POOL_HIST + 1 - w + T]
        cnt = jnp.minimum(w, pos + 1).astype(jnp.float32)[None, :, None]
        outs.append(win_sum / cnt - uf[:, :, sl])
    d = jnp.stack(outs, axis=2).astype(u.dtype)
    y = jnp.einsum('btgc,gcd->btgd', d, w_pool_group).reshape(B, T, D_MODEL)
    return y * pool_scale


def _layer(x, pool_hist, pos0, past, ffn1_norm, ffn1_w_gate, ffn1_w_up, ffn1_w_down, mix_norm, w_in, b_forget,
           w_branch_attn, w_pool_group, pool_scale, w_out, ffn2_norm, ffn2_w_gate, ffn2_w_up, ffn2_w_down):
    B, T, _ = x.shape
    x = x + 0.5 * _half_ffn(x, ffn1_norm, ffn1_w_gate, ffn1_w_up, ffn1_w_down)
    h = _rmsnorm(x, mix_norm)
    z = h @ w_in
    q, k, v, f_lin, u, g_a, g_b = jnp.split(z, IN_SPLITS, axis=-1)
    q = q.reshape(B, T, N_HEADS_A, HEAD_DIM_A)
    k = k.reshape(B, T, N_HEADS_A, HEAD_DIM_A)
    v = v.reshape(B, T, N_HEADS_A, HEAD_DIM_A)
    logf = jax.nn.log_sigmoid(f_lin.astype(jnp.float32) + b_forget.astype(jnp.float32))
    if past is None:
        o = _fox_prompt(q, k, v, jnp.cumsum(logf, axis=1))
    else:
        o = _fox_sample(q, k, v, logf, past[0], past[1], past[2])
    attn_out = o @ w_branch_attn
    pool_out = _pool_branch(u, pool_hist, pos0, w_pool_group, pool_scale)
    mixed = jax.nn.sigmoid(g_a) * attn_out + jax.nn.sigmoid(g_b) * pool_out
    x = x + mixed @ w_out
    x = x + 0.5 * _half_ffn(x, ffn2_norm, ffn2_w_gate, ffn2_w_up, ffn2_w_down)
    new_pool = jnp.concatenate([pool_hist, u], axis=1)[:, -POOL_HIST:]
    return x, k, v, logf, new_pool


def setup_inputs(seed: int = 0) -> dict:
    key = jax.random.key(seed)
    ks = jax.random.split(key, 24)
    f32 = jnp.float32

    def nrm(k, shape, scale):
        return jax.random.normal(k, shape, f32) * scale

    def gain(k, shape):
        return 1.0 + 0.02 * jax.random.normal(k, shape, f32)

    L = DEPTH
    return {
        "x_prompt": nrm(ks[0], (BATCH, SEQ, D_MODEL), 1.0),
        "x_sample": nrm(ks[1], (DEC_BATCH, DEC_SEQ, D_MODEL), 1.0),
        "cache_k": nrm(ks[2], (L, DEC_BATCH, PAST_LEN, N_HEADS_A, HEAD_DIM_A), 1.0),
        "cache_v": nrm(ks[3], (L, DEC_BATCH, PAST_LEN, N_HEADS_A, HEAD_DIM_A), 1.0),
        "cache_logf": jax.nn.log_sigmoid(3.0 + jax.random.normal(ks[4], (L, DEC_BATCH, PAST_LEN, N_HEADS_A), f32)),
        "state_pool": nrm(ks[5], (L, DEC_BATCH, POOL_HIST, POOL_WIDTH), 1.0),
        "ffn1_norm": gain(ks[6], (L, D_MODEL)),
        "ffn1_w_gate": nrm(ks[7], (L, D_MODEL, D_FF), D_MODEL ** -0.5),
        "ffn1_w_up": nrm(ks[8], (L, D_MODEL, D_FF), D_MODEL ** -0.5),
        "ffn1_w_down": nrm(ks[9], (L, D_FF, D_MODEL), D_FF ** -0.5),
        "mix_norm": gain(ks[10], (L, D_MODEL)),
        "w_in": nrm(ks[11], (L, D_MODEL, IN_WIDTH), D_MODEL ** -0.5),
        "b_forget": 3.0 + 0.1 * jax.random.normal(ks[12], (L, N_HEADS_A), f32),
        "w_branch_attn": nrm(ks[13], (L, ATTN_WIDTH, D_MODEL), ATTN_WIDTH ** -0.5),
        "w_pool_group": nrm(ks[14], (L, N_POOL_GROUPS, POOL_GROUP, POOL_OUT_GROUP), POOL_GROUP ** -0.5),
        "pool_scale": gain(ks[15], (L, D_MODEL)),
        "w_out": nrm(ks[16], (L, D_MODEL, D_MODEL), D_MODEL ** -0.5),
        "ffn2_norm": gain(ks[17], (L, D_MODEL)),
        "ffn2_w_gate": nrm(ks[18], (L, D_MODEL, D_FF), D_MODEL ** -0.5),
        "ffn2_w_up": nrm(ks[19], (L, D_MODEL, D_FF), D_MODEL ** -0.5),
        "ffn2_w_down": nrm(ks[20], (L, D_FF, D_MODEL), D_FF ** -0.5),
        "final_norm": gain(ks[21], (D_MODEL,)),
    }


def reference(x_prompt, x_sample, cache_k, cache_v, cache_logf, state_pool, ffn1_norm, ffn1_w_gate, ffn1_w_up,
              ffn1_w_down, mix_norm, w_in, b_forget, w_branch_attn, w_pool_group, pool_scale, w_out, ffn2_norm,
              ffn2_w_gate, ffn2_w_up, ffn2_w_down, final_norm):
    hp, hs = x_prompt, x_sample
    kp, vp, lp, pp, ksm, vsm, lsm, psm = [], [], [], [], [], [], [], []
    past_len = cache_k.shape[2]
    for l in range(DEPTH):
        w = (ffn1_norm[l], ffn1_w_gate[l], ffn1_w_up[l], ffn1_w_down[l], mix_norm[l], w_in[l], b_forget[l],
             w_branch_attn[l], w_pool_group[l], pool_scale[l], w_out[l], ffn2_norm[l], ffn2_w_gate[l],
             ffn2_w_up[l], ffn2_w_down[l])
        zero_hist = jnp.zeros((hp.shape[0], POOL_HIST, POOL_WIDTH), hp.dtype)
        hp, k1, v1, l1, p1 = _layer(hp, zero_hist, 0, None, *w)
        hs, k2, v2, l2, p2 = _layer(hs, state_pool[l], past_len, (cache_k[l], cache_v[l], cache_logf[l]), *w)
        kp.append(k1); vp.append(v1); lp.append(l1); pp.append(p1)
        ksm.append(k2); vsm.append(v2); lsm.append(l2); psm.append(p2)
    y_prompt = _rmsnorm(hp, final_norm)
    y_sample = _rmsnorm(hs, final_norm)
    k_prompt = jnp.stack(kp)
    v_prompt = jnp.stack(vp)
    logf_prompt = jnp.stack(lp)
    pool_prompt = jnp.stack(pp)
    k_sample = jnp.stack(ksm)
    v_sample = jnp.stack(vsm)
    logf_sample = jnp.stack(lsm)
    pool_sample = jnp.stack(psm)
    return (y_prompt, y_sample, k_prompt, v_prompt, logf_prompt, pool_prompt, k_sample, v_sample, logf_sample, pool_sample)
```

```python
import bisect
import numpy as np
import concourse.bass as bass
import concourse.mybir as mybir

F32 = mybir.dt.float32
BF16 = mybir.dt.bfloat16
U8 = mybir.dt.uint8
AF = mybir.ActivationFunctionType
ALU = mybir.AluOpType
DTSIZE = {F32: 4, BF16: 2, U8: 1}

ENGS = ("pe", "act", "dve", "pool", "sp")


class Op:
    __slots__ = ("eng", "fn", "waits", "sig", "sigval", "dma_key", "idx", "needs_sig")

    def __init__(self, eng, fn, dma_key=None):
        self.eng = eng
        self.fn = fn
        self.waits = {}
        self.needs_sig = False
        self.sigval = None
        self.dma_key = dma_key
        self.idx = -1


class IntervalMap:
    def __init__(self, size):
        self.starts = [0]
        self.segs = {0: [size, None, {}]}

    def _split(self, pos):
        i = bisect.bisect_right(self.starts, pos) - 1
        s = self.starts[i]
        seg = self.segs[s]
        if s == pos or pos >= seg[0]:
            return
        end = seg[0]
        seg[0] = pos
        self.segs[pos] = [end, seg[1], dict(seg[2])]
        self.starts.insert(i + 1, pos)

    def access(self, lo, hi, op, write):
        self._split(lo)
        self._split(hi)
        i = bisect.bisect_left(self.starts, lo)
        deps = []
        j = i
        while j < len(self.starts) and self.starts[j] < hi:
            seg = self.segs[self.starts[j]]
            if seg[1] is not None:
                deps.append(seg[1])
            if write:
                deps.extend(seg[2].values())
            j += 1
        if write:
            for k in range(i + 1, j):
                del self.segs[self.starts[k]]
            del self.starts[i + 1:j]
            self.segs[lo] = [hi, op, {}]
        else:
            rk = (op.eng, op.dma_key)
            for k in range(i, j):
                self.segs[self.starts[k]][2][rk] = op
        return deps


class View:
    __slots__ = ("ap", "arena", "lo", "hi")

    def __init__(self, ap, arena, lo, hi):
        self.ap = ap
        self.arena = arena
        self.lo = lo
        self.hi = hi


class Buf:
    def __init__(self, sched, arena, off, shape, dtype, ap):
        self.s = sched
        self.arena = arena
        self.off = off
        self.shape = tuple(shape)
        self.dtype = dtype
        self.ap = ap
        self.esz = DTSIZE[dtype]
        st = [1]
        for d in reversed(self.shape[2:]):
            st.insert(0, st[0] * d)
        self.strides = st
        self.nbytes = int(np.prod(self.shape[1:])) * self.esz

    def __getitem__(self, key):
        if not isinstance(key, tuple):
            key = (key,)
        ap = self.ap[key]
        fk = list(key[1:]) + [slice(None)] * (len(self.shape) - len(key))
        lo = 0
        hi = 0
        for k, dim, st in zip(fk, self.shape[1:], self.strides):
            if isinstance(k, slice):
                a, b, step = k.indices(dim)
                assert step == 1
                lo += a * st
                hi += (b - 1) * st
            else:
                lo += k * st
                hi += k * st
        return View(ap, self.arena, self.off + lo * self.esz, self.off + (hi + 1) * self.esz)

    def all(self):
        return self[tuple(slice(None) for _ in self.shape)]


class Sched:
    def __init__(self, nc):
        self.nc = nc
        self.ops = {e: [] for e in ENGS}
        self.maps = {}
        self.nops = 0
        self.dma_counts = {}
        self.dma_last = {}
        self.arenas = {}
        self.same_eng_sync = ("act", "dve", "pool")
        self.relax_same_eng_waw = False

    def add_arena(self, name, size):
        self.maps[name] = IntervalMap(size)

    def add_sbuf_arena(self, name, handle, size):
        self.arenas[name] = [handle, size, 0]
        self.add_arena(name, size)

    def alloc(self, arena, shape, dtype, at=None, align=64):
        h, size, cur = self.arenas[arena]
        esz = DTSIZE[dtype]
        nb = int(np.prod(shape[1:])) * esz
        if at is None:
            at = (cur + align - 1) // align * align
            self.arenas[arena][2] = at + nb
        assert at + nb <= size, (arena, at, nb, size)
        ap = h[0:shape[0], at:at + nb]
        if dtype != U8:
            ap = ap.bitcast(dtype)
        if len(shape) > 2:
            names = " ".join("d%d" % i for i in range(len(shape) - 1))
            kw = {"d%d" % i: shape[i + 1] for i in range(1, len(shape) - 1)}
            ap = ap.rearrange("p (%s) -> p %s" % (names, names), **kw)
        return Buf(self, arena, at, shape, dtype, ap)

    def cursor(self, arena):
        return self.arenas[arena][2]

    def set_cursor(self, arena, v):
        self.arenas[arena][2] = v

    def _add(self, op, reads, writes):
        op.idx = self.nops
        self.nops += 1
        rdeps = []
        wdeps = []
        for v in reads:
            rdeps.extend(self.maps[v.arena].access(v.lo, v.hi, op, False))
        for v in writes:
            wdeps.extend(self.maps[v.arena].access(v.lo, v.hi, op, True))
        for israw, deps in ((True, rdeps), (False, wdeps)):
            for d in deps:
                if d is op:
                    continue
                if d.eng == op.eng and d.dma_key is None and op.dma_key is None:
                    if op.eng not in self.same_eng_sync:
                        continue
                    if (not israw) and self.relax_same_eng_waw:
                        continue
                d.needs_sig = True
                op.waits[id(d)] = d
        self.ops[op.eng].append(op)
        return op

    def op(self, eng, fn, reads=(), writes=()):
        return self._add(Op(eng, fn), reads, writes)

    def dma(self, eng, key, out, in_, reads=(), writes=(), **kw):
        oap = out.ap if isinstance(out, View) else out
        iap = in_.ap if isinstance(in_, View) else in_
        rd = list(reads) + ([in_] if isinstance(in_, View) else [])
        wr = list(writes) + ([out] if isinstance(out, View) else [])

        def fn(e):
            return e.dma_start(out=oap, in_=iap, **kw)
        op = Op(eng, fn, dma_key=key)
        prev = self.dma_last.get(key)
        self._add(op, rd, wr)
        if prev is not None:
            op.waits[id(prev)] = prev
        self.dma_last[key] = op
        op.needs_sig = True
        return op

    def prepare(self):
        for e in ENGS:
            cnt = 0
            for op in self.ops[e]:
                if op.dma_key is not None:
                    c = self.dma_counts.get(op.dma_key, 0) + 1
                    self.dma_counts[op.dma_key] = c
                    op.sigval = ("d", op.dma_key, 16 * c)
                elif op.needs_sig:
                    cnt += 1
                    op.sigval = ("e", e, cnt)

    def emit_engine(self, e, eng, sems, dma_sems, final_keys=None):
        seen = {}
        nw = 0
        for op in self.ops[e]:
            need = {}
            for d in op.waits.values():
                kind, k, v = d.sigval
                kk = (kind, k)
                if seen.get(kk, 0) >= v:
                    continue
                if need.get(kk, 0) < v:
                    need[kk] = v
            for kk, v in need.items():
                sem = sems[kk[1]] if kk[0] == "e" else dma_sems[kk[1]]
                eng.wait_ge(sem, v)
                seen[kk] = v
                nw += 1
            ins = op.fn(eng)
            if op.dma_key is not None:
                ins.then_inc(dma_sems[op.dma_key], 16)
            elif op.needs_sig:
                ins.then_inc(sems[e], 1)
        if final_keys is not None:
            for k in final_keys:
                c = self.dma_counts.get(k, 0)
                if c:
                    eng.wait_ge(dma_sems[k], 16 * c)
        return (len(self.ops[e]), nw)
import math
from concourse.bass_utils import run_bass_kernel_spmd

D = 2048
KC = 16
H = 8
DH = 128
AW = 1024
PW = 1024
INW = 8200
C_Q, C_K, C_V, C_F, C_U, C_GA, C_GB = 0, 1024, 2048, 3072, 3080, 4104, 6152
WINS = (2, 4, 8, 16)
HIST = 15
EPS = 1e-6
SLOT = 4096
NSLOT = 3
NTMP = 6
ARENA = 211968


class Cfg:
    def __init__(self, nseq_p=2, S_=2048, nseq_s=2, TD=64, PAST=2048, DFF=5632):
        self.nseq_p, self.S, self.nseq_s, self.TD, self.PAST, self.DFF = nseq_p, S_, nseq_s, TD, PAST, DFF
        self.NJ = DFF // 128
        self.NJH = self.NJ // 2
        assert self.NJ % 2 == 0 and S_ % 512 == 0 and PAST % 128 == 0 and nseq_s * TD == 128


def jgroups(n):
    g = []
    a = 0
    while a < n:
        b = min(n, a + 8)
        g.append((a, b))
        a = b
    return g


def block_list(cfg):
    bl = []
    for f in (1, 2):
        for half in range(2):
            for jj in range(cfg.NJH):
                j = half * cfg.NJH + jj
                bl.append((("GU", f, j), [(0, "wg%d" % f, 0, 16, j * 128, 128), (2048, "wu%d" % f, 0, 16, j * 128, 128)]))
            for n in range(4):
                for (a, b) in jgroups(cfg.NJH):
                    bl.append((("D", f, half, n, a), [(0, "wd%d" % f, half * cfg.NJH + a, b - a, n * 512, 512)]))
        if f == 1:
            for hp in range(4):
                bl.append((("Q", hp), [(0, "win", 0, 16, C_Q + hp * 256, 128), (2048, "win", 0, 16, C_Q + hp * 256 + 128, 128)]))
            for n in range(2):
                for half in range(2):
                    bl.append((("K", n, half), [(0, "win", half * 8, 8, C_K + n * 512, 512)]))
            for n in range(2):
                for half in range(2):
                    bl.append((("V", n, half), [(0, "win", half * 8, 8, C_V + n * 512, 512)]))
            bl.append((("F",), [(0, "win", 0, 16, C_F, 8)]))
            for up in range(4):
                bl.append((("U", up), [(0, "win", 0, 16, C_U + up * 256, 128), (2048, "win", 0, 16, C_U + up * 256 + 128, 128)]))
            for c in range(16):
                g = c // 4
                bl.append((("AB", c), [(0, "wba", 0, 8, c * 128, 128), (1024, "wp%d" % g, 0, 2, (c % 4) * 128, 128)]))
                bl.append((("G", c), [(0, "win", 0, 16, C_GA + c * 128, 128), (2048, "win", 0, 16, C_GB + c * 128, 128)]))
            for n in range(4):
                for half in range(2):
                    bl.append((("WO", n, half), [(0, "wo", half * 8, 8, n * 512, 512)]))
    return bl


class StopBuild(Exception):
    pass


def build_program(cfg):
    nc = bass.Bass("TRN2", target_bir_lowering=False)
    stop_at = getattr(cfg, "stop", None)

    def chk(name):
        if stop_at == name:
            raise StopBuild(name)
    DFF = cfg.DFF

    def din(name, shape):
        return nc.dram_tensor(name, list(shape), F32, kind="ExternalInput")

    def dout(name, shape):
        return nc.dram_tensor(name, list(shape), F32, kind="ExternalOutput")

    T = {}
    T["xp"] = din("xp", [cfg.nseq_p, cfg.S, D])
    T["xs"] = din("xs", [cfg.nseq_s, cfg.TD, D])
    T["ck"] = din("ck", [cfg.nseq_s, cfg.PAST, AW])
    T["cv"] = din("cv", [cfg.nseq_s, cfg.PAST, AW])
    T["cl"] = din("cl", [cfg.nseq_s, cfg.PAST, H])
    T["spool"] = din("spool", [cfg.nseq_s, HIST, PW])
    for f in (1, 2):
        T["n%d" % f] = din("n%d" % f, [D])
        T["wg%d" % f] = din("wg%d" % f, [D, DFF])
        T["wu%d" % f] = din("wu%d" % f, [D, DFF])
        T["wd%d" % f] = din("wd%d" % f, [DFF, D])
    T["nm"] = din("nm", [D])
    T["win"] = din("win", [D, INW])
    T["bf"] = din("bf", [H])
    T["wba"] = din("wba", [AW, D])
    for g in range(4):
        T["wp%d" % g] = din("wp%d" % g, [256, 512])
    T["psc"] = din("psc", [D])
    T["wo"] = din("wo", [D, D])
    T["nf"] = din("nf", [D])
    T["yp"] = dout("yp", [cfg.nseq_p, cfg.S, D])
    T["ys"] = dout("ys", [cfg.nseq_s, cfg.TD, D])
    T["kp"] = dout("kp", [cfg.nseq_p, cfg.S, AW])
    T["vp"] = dout("vp", [cfg.nseq_p, cfg.S, AW])
    T["lp"] = dout("lp", [cfg.nseq_p, cfg.S, H])
    T["pp"] = dout("pp", [cfg.nseq_p, HIST, PW])
    T["ks"] = dout("ks", [cfg.nseq_s, cfg.TD, AW])
    T["vs"] = dout("vs", [cfg.nseq_s, cfg.TD, AW])
    T["ls"] = dout("ls", [cfg.nseq_s, cfg.TD, H])
    T["pso"] = dout("pso", [cfg.nseq_s, HIST, PW])
    if getattr(cfg, "debug", False):
        for nm_, shp, dt_ in (("dbg_qT", [128, 8 * 512], BF16), ("dbg_oT", [128, 8 * 512], BF16), ("dbg_dT", [128, 8 * 512], BF16),
                              ("dbg_mT", [128, 16 * 512], BF16), ("dbg_negc", [128, 4 * H], F32), ("dbg_cTb", [8, 512], BF16),
                              ("dbg_KT", [128, 8 * 512], BF16), ("dbg_V", [128, 4 * AW], BF16)):
            T[nm_] = nc.dram_tensor(nm_, shp, dt_, kind="ExternalOutput")
    A = {k: v.ap() for k, v in T.items()}

    blocks = block_list(cfg)
    NB = len(blocks)
    wsc_t = nc.dram_tensor("wsc", [NB, 128, SLOT], BF16, kind="Internal")
    wsc = wsc_t.ap()

    import contextlib
    es = contextlib.ExitStack()
    with es:
        arena = es.enter_context(nc.sbuf_tensor("arena", [128, ARENA], U8))
        psb = [es.enter_context(nc.psum_tensor("ps%d" % i, [128, 512], F32)) for i in range(8)]
        sem_e = {e: es.enter_context(nc.semaphore("s_" + e)) for e in ENGS}
        dkeys = (["w%d" % i for i in range(NSLOT)] + ["x%d" % i for i in range(4)] + ["o%d" % i for i in range(NTMP)]
                 + ["pl%d" % i for i in range(6)] + ["ps%d" % i for i in range(6)] + ["wb%d" % i for i in range(8)]
                 + ["c0", "c1", "c2", "c3", "misc", "sh0", "sh1", "lf"] + ["y%d" % i for i in range(8)])
        sem_d = {k: es.enter_context(nc.semaphore("d_" + k)) for k in dkeys}
        block = es.enter_context(nc.Block())

        S = Sched(nc)
        S.add_sbuf_arena("sb", arena, ARENA)
        S.add_arena("wsc", NB * SLOT)
        PS = []
        PSB = []
        for i in range(8):
            S.add_arena("ps%d" % i, 2048)
            PS.append(Buf(S, "ps%d" % i, 0, [128, 512], F32, psb[i][:, :]))
            PSB.append(Buf(S, "ps%d" % i, 0, [128, 1024], BF16, psb[i][:, :].bitcast(BF16)))
        bank_ctr = [0]

        def nb_(n=1):
            r = []
            for _ in range(n):
                r.append(bank_ctr[0] % 8)
                bank_ctr[0] += 1
            return r if n > 1 else r[0]

        def nb_excl(excl):
            while True:
                b = nb_()
                if b not in excl:
                    return b

        xt = S.alloc("sb", [128, 4, D], F32)
        hT = S.alloc("sb", [128, KC, 512], BF16)
        R0 = S.cursor("sb")
        xn = S.alloc("sb", [128, 4, D], BF16, at=R0)
        hid = S.alloc("sb", [128, 22, 512], BF16, at=R0)
        dT = S.alloc("sb", [128, 8, 512], BF16, at=R0)
        oT = S.alloc("sb", [128, 8, 512], BF16, at=R0 + 8192)
        ktok = S.alloc("sb", [128, 4, AW], BF16, at=R0 + 8192)
        qT = S.alloc("sb", [128, 8, 512], BF16, at=R0 + 16384)
        mT = S.alloc("sb", [128, 16, 512], BF16, at=R0 + 16384)
        S.set_cursor("sb", R0 + 32768)
        KTW = max(cfg.S, cfg.PAST, 2048)
        NKT = KTW // 128
        KT = S.alloc("sb", [128, H, KTW], BF16)
        Vc = S.alloc("sb", [128, NKT + 1, AW], BF16)
        wsl = [S.alloc("sb", [128, SLOT], BF16) for _ in range(NSLOT)]
        tmp = [S.alloc("sb", [128, 512], F32) for _ in range(NTMP)]
        ub = [S.alloc("sb", [128, 15 + 512], F32, at=R0 + 8192)]
        pa = [S.alloc("sb", [128, 15 + 512], F32, at=R0 + 8192 + 2112 * (i + 1)) for i in range(2)]
        hist = S.alloc("sb", [128, 8, HIST], F32)
        NPT = 4
        PT = [S.alloc("sb", [128, 512], BF16) for _ in range(NPT)]
        sg = [S.alloc("sb", [128, 512], BF16) for _ in range(2)]
        logfc = S.alloc("sb", [128, NKT + 1, H], F32)
        negc = S.alloc("sb", [128, NKT + 1, H], F32)
        ccol = S.alloc("sb", [128, H], F32)
        cTb = S.alloc("sb", [128, KTW + 128], BF16)
        ktn = S.alloc("sb", [128, H, 128], BF16, at=16384)
        vnew = S.alloc("sb", [128, AW], BF16, at=18432)
        lpast = S.alloc("sb", [128, 2, NKT, H], F32, at=20480)
        ident = S.alloc("sb", [128, 128], BF16)
        identf = S.alloc("sb", [128, 128], F32)
        onesb = S.alloc("sb", [128, 128], BF16)
        onesf = S.alloc("sb", [128, 128], F32)
        trif = S.alloc("sb", [128, 128], F32)
        trib = S.alloc("sb", [128, 128], BF16)
        triblk = S.alloc("sb", [128, 128], F32)
        halfsel = S.alloc("sb", [128, 2, 128], F32)
        sel = S.alloc("sb", [128, H, 128], BF16)
        self_f = S.alloc("sb", [128, H, 128], F32, at=tmp[0].off)
        gT = S.alloc("sb", [128, 3, KC], F32)
        pscT = S.alloc("sb", [128, KC], F32)
        bfb = S.alloc("sb", [128, H], F32)
        epsb = S.alloc("sb", [128, 1], F32)
        ssq = S.alloc("sb", [128, 4], F32)
        ssqp = S.alloc("sb", [128, 4, 4], F32)
        junks = [S.alloc("sb", [128, 512], BF16) for _ in range(4)]
        rstd = S.alloc("sb", [128, 4], F32)
        fixw = S.alloc("sb", [128, 4, 16], F32)
        fl = S.alloc("sb", [128, 4, H], F32)
        ltot = S.alloc("sb", [128, H], F32)
        lptot = S.alloc("sb", [128, 2, H], F32)
        print("sbuf used", S.cursor("sb"), "of", ARENA)
        NPS = 6
        PCE = 1024
        pbase = Vc.off + 4 * 2048
        assert NKT + 1 >= 16
        pst = [S.alloc("sb", [128, PCE], F32, at=pbase + i * 4096) for i in range(NPS)]
        cst = [S.alloc("sb", [128, AW], F32, at=8192 + i * 4096) for i in range(2)]
        cbf = [S.alloc("sb", [128, AW], BF16, at=8192 + 16384 + i * 2048) for i in range(2)]
        hst = S.alloc("sb", [16, PW], F32, at=8192 + 16384 + 4096)

        tmp_ctr = [0]

        def ntmp():
            i = tmp_ctr[0] % NTMP
            tmp_ctr[0] += 1
            return i

        rot = {}

        def nxt(name, n):
            rot[name] = (rot.get(name, -1) + 1) % n
            return rot[name]

        def mm(out, lhsT, rhs, start, stop):
            S.op("pe", lambda e: e.matmul(out.ap, lhsT.ap, rhs.ap, start=start, stop=stop), [lhsT, rhs], [out])

        def tr(out, in_, idn):
            S.op("pe", lambda e: e.transpose(out.ap, in_.ap, idn.ap), [in_, idn], [out])

        def act(out, in_, func, bias=None, scale=None, accum=None, extra_w=()):
            kw = {}
            rd = [in_]
            if bias is not None:
                kw["bias"] = bias.ap if isinstance(bias, View) else bias
                if isinstance(bias, View):
                    rd.append(bias)
            if scale is not None:
                kw["scale"] = scale.ap if isinstance(scale, View) else scale
                if isinstance(scale, View):
                    rd.append(scale)
            wr = [out]
            if accum is not None:
                kw["accum_out"] = accum.ap
                wr.append(accum)
            S.op("act", lambda e: e.activation(out.ap, in_.ap, func, **kw), rd, wr)

        def tt(eng, out, in0, in1, op):
            S.op(eng, lambda e: e.tensor_tensor(out.ap, in0.ap, in1.ap, op), [in0, in1], [out])

        def ts(eng, out, in0, s1, op0, s2=None, op1=None):
            rd = [in0]
            a1 = s1.ap if isinstance(s1, View) else s1
            if isinstance(s1, View):
                rd.append(s1)
            if op1 is None:
                S.op(eng, lambda e: e.tensor_scalar(out.ap, in0.ap, a1, None, op0), rd, [out])
            else:
                a2 = s2.ap if isinstance(s2, View) else s2
                if isinstance(s2, View):
                    rd.append(s2)
                S.op(eng, lambda e: e.tensor_scalar(out.ap, in0.ap, a1, a2, op0, op1), rd, [out])

        def stt(eng, out, in0, sc, in1, op0, op1):
            rd = [in0, in1]
            a = sc.ap if isinstance(sc, View) else sc
            if isinstance(sc, View):
                rd.append(sc)
            S.op(eng, lambda e: e.scalar_tensor_tensor(out.ap, in0.ap, a, in1.ap, op0, op1), rd, [out])

        def cp(eng, out, in_):
            if eng == "act":
                S.op("act", lambda e: e.copy(out.ap, in_.ap), [in_], [out])
            else:
                S.op(eng, lambda e: e.tensor_copy(out.ap, in_.ap), [in_], [out])

        def memset(eng, out, val):
            S.op(eng, lambda e: e.memset(out.ap, val), [], [out])

        def asel(out, pattern, cmp, fill, base, cm):
            S.op("pool", lambda e: e.affine_select(out.ap, out.ap, pattern, cmp, fill, base=base, channel_multiplier=cm), [out], [out])

        def sub3(view_buf, off, a, b):
            v = view_buf[:, off:off + a * b]
            return View(v.ap.rearrange("p (a b) -> p a b", b=b), v.arena, v.lo, v.hi)

        memset("pool", identf.all(), 1.0)
        asel(identf.all(), [[-1, 128]], ALU.is_equal, 0.0, 0, 1)
        cp("dve", ident.all(), identf.all())
        memset("pool", onesf.all(), 1.0)
        memset("pool", onesb.all(), 1.0)
        memset("pool", trif.all(), 1.0)
        asel(trif.all(), [[1, 128]], ALU.is_ge, 0.0, 0, -1)
        cp("dve", trib.all(), trif.all())
        cp("dve", triblk.all(), trif.all())
        S.op("pool", lambda e: e.memset(triblk[0:64, 64:128].ap, 0.0), [], [triblk.all()])
        memset("pool", halfsel.all(), 1.0)
        S.op("pool", lambda e: e.memset(halfsel[:, 0, 64:128].ap, 0.0), [], [halfsel.all()])
        S.op("pool", lambda e: e.memset(halfsel[:, 1, 0:64].ap, 0.0), [], [halfsel.all()])
        memset("pool", self_f.all(), 1.0)
        asel(self_f.all(), [[-1, H], [0, 128]], ALU.is_equal, 0.0, 0, 1)
        cp("dve", sel.all(), self_f.all())
        memset("pool", cTb.all(), 0.0)
        memset("pool", epsb.all(), EPS)
        for g, w in enumerate(WINS):
            for t in range(16):
                v = float(w) / float(min(w, t + 1))
                S.op("pool", (lambda g=g, t=t, v=v: (lambda e: e.memset(fixw[:, g, t:t + 1].ap, v)))(), [], [fixw[:, g, t:t + 1]])
        for i, nm_ in enumerate(("n1", "nm", "n2")):
            src = bass.AP(T[nm_], 0, [[1, 128], [128, KC]])
            S.dma("sp", "misc", gT[:, i, :], src, allow_slow_non_contiguous=True)
        S.dma("sp", "misc", pscT.all(), bass.AP(T["psc"], 0, [[1, 128], [128, KC]]), allow_slow_non_contiguous=True)
        S.dma("sp", "misc", bfb.all(), bass.AP(T["bf"], 0, [[0, 128], [1, H]]))

        pieces = []
        for (name, parts) in blocks:
            pl = []
            for (off, wn, r0, nr, c0, wd) in parts:
                step = max(1, PCE // wd)
                k0 = 0
                while k0 < nr:
                    nk = min(step, nr - k0)
                    pl.append((off + k0 * wd, wn, r0 + k0, nk, c0, wd))
                    k0 += nk
            pieces.append(pl)
        used_of = []
        for (name, parts) in blocks:
            used_of.append(max(off + nr * wd for (off, wn, r0, nr, c0, wd) in parts))
        cstate = {"blk": 0, "unit": 0}

        def convert_block(bi, wslot):
            for (off, wn, r0, nk, c0, wd) in pieces[bi]:
                u = cstate["unit"]
                sl = u % NPS
                ne = nk * wd
                src = A[wn][r0 * 128:(r0 + nk) * 128, c0:c0 + wd].rearrange("(k p) w -> p k w", p=128)
                dst = sub3(pst[sl], 0, nk, wd)
                S.dma("sp", "pl%d" % sl, dst, src)
                cp("act" if u % 2 == 0 else "dve", wslot[:, off:off + ne], pst[sl][:, 0:ne])
                S.dma("pool", "ps%d" % sl, wsc[bi, :, off:off + ne], wslot[:, off:off + ne],
                      writes=[View(None, "wsc", bi * SLOT + off, bi * SLOT + off + ne)])
                cstate["unit"] += 1

        CLOOK = 2
        wstate = {"issued": 0, "next": 0, "done": 0}
        NPASS = cfg.nseq_p * (cfg.S // 512) + 1
        TOTAL = NPASS * (NB + 1)

        NBIG = 8
        bigsl = [S.alloc("sb", [128, SLOT], BF16, at=(KT.off + i * 8192) if i < 4 else (Vc.off + (i - 4) * 8192)) for i in range(NBIG)]
        slot_last = {}
        slot_map = {}
        ring_ctr = {"s": 0, "b": 0}
        MAXLA = 8

        def issue_to(g):
            while wstate["issued"] <= min(g, TOTAL - 1):
                gi = wstate["issued"]
                bi = gi % (NB + 1)
                big = (gi // (NB + 1) == NPASS - 1) and bi < NB and blocks[bi][0][0] in ("GU", "D") and getattr(cfg, "bigring", True)
                if big:
                    skey = ("b", ring_ctr["b"] % NBIG)
                    buf = bigsl[skey[1]]
                    dkey = "wb%d" % skey[1]
                else:
                    skey = ("s", ring_ctr["s"] % NSLOT)
                    buf = wsl[skey[1]]
                    dkey = "w%d" % skey[1]
                if slot_last.get(skey, -1) >= wstate["done"]:
                    break
                if bi == NB:
                    v = buf.all()
                    dstv = View(v.ap.bitcast(F32), v.arena, v.lo, v.hi)
                    S.dma("sp", dkey, dstv, bass.AP(T["nf"], 0, [[0, 128], [1, D]]))
                else:
                    u = used_of[bi]
                    if gi < NB:
                        convert_block(bi, buf)
                    else:
                        S.dma("sp", dkey, buf[:, 0:u], wsc[bi, :, 0:u], reads=[View(None, "wsc", bi * SLOT, bi * SLOT + u)])
                ring_ctr["b" if big else "s"] += 1
                slot_last[skey] = gi
                slot_map[gi] = buf
                wstate["issued"] += 1

        def wget(name):
            gi = wstate["next"]
            bi = gi % (NB + 1)
            if name == "GF":
                assert bi == NB, (name, bi)
            else:
                assert blocks[bi][0] == name, (name, blocks[bi][0])
            issue_to(wstate["done"] + MAXLA - 1)
            assert gi < wstate["issued"], "weight block not issued (too many live blocks?)"
            wstate["next"] += 1
            return slot_map.pop(gi)

        def wdone(n=1):
            wstate["done"] += n
            assert wstate["done"] <= wstate["next"]
            issue_to(wstate["done"] + MAXLA - 1)

        def sq_block(s_, n_):
            act(junks[nxt("junk", 4)].all(), xt[:, s_, n_ * 512:(n_ + 1) * 512], AF.Square, accum=ssqp[:, s_, n_:n_ + 1])

        def rstd_from_parts(NT):
            S.op("dve", lambda e: e.tensor_reduce(ssq[:, 0:NT].ap, ssqp[:, 0:NT, :].ap, mybir.AxisListType.X, ALU.add),
                 [ssqp[:, 0:NT, :]], [ssq[:, 0:NT]])
            act(rstd[:, 0:NT], ssq[:, 0:NT], AF.Ln, bias=epsb.all(), scale=1.0 / D)
            act(rstd[:, 0:NT], rstd[:, 0:NT], AF.Exp, scale=-0.5)

        def norm_T(NT, gi):
            N = NT * 128
            rstd_from_parts(NT)
            for s in range(NT):
                if s < 2:
                    ts("dve", xn[:, s, :], xt[:, s, :], rstd[:, s:s + 1], ALU.mult)
                else:
                    S.op("act", (lambda s=s: (lambda e: e.mul(xn[:, s, :].ap, xt[:, s, :].ap, rstd[:, s:s + 1].ap)))(),
                         [xt[:, s, :], rstd[:, s:s + 1]], [xn[:, s, :]])
            for c in range(KC):
                b = nb_()
                for s in range(NT):
                    tr(PSB[b][:, s * 128:(s + 1) * 128], xn[:, s, c * 128:(c + 1) * 128], ident.all())
                if c % 2 == 0:
                    ts("dve", hT[:, c, 0:N], PSB[b][:, 0:N], gT[:, gi, c:c + 1], ALU.mult)
                else:
                    S.op("act", (lambda c=c, b=b: (lambda e: e.mul(hT[:, c, 0:N].ap, PSB[b][:, 0:N].ap, gT[:, gi, c:c + 1].ap)))(),
                         [PSB[b][:, 0:N], gT[:, gi, c:c + 1]], [hT[:, c, 0:N]])

        def ffn(NT, f, gi):
            N = NT * 128
            norm_T(NT, gi)
            for half in range(2):
                for jj in range(cfg.NJH):
                    j = half * cfg.NJH + jj
                    w = wget(("GU", f, j))
                    wg = sub3(w, 0, 16, 128)
                    wu = sub3(w, 2048, 16, 128)
                    bA, bB = nb_(2)
                    for kc in range(KC):
                        mm(PS[bA][:, 0:N], View(wg.ap[:, kc, :], wg.arena, wg.lo, wg.hi), hT[:, kc, 0:N], kc == 0, kc == KC - 1)
                    for kc in range(KC):
                        mm(PS[bB][:, 0:N], View(wu.ap[:, kc, :], wu.arena, wu.lo, wu.hi), hT[:, kc, 0:N], kc == 0, kc == KC - 1)
                    si = nxt("sg", 2)
                    act(sg[si][:, 0:N], PS[bA][:, 0:N], AF.Silu)
                    tt("dve", hid[:, jj, 0:N], PS[bB][:, 0:N], sg[si][:, 0:N], ALU.mult)
                    wdone()
                for n in range(4):
                    acc = nb_(4)
                    for (a, b) in jgroups(cfg.NJH):
                        w = wget(("D", f, half, n, a))
                        wv = sub3(w, 0, b - a, 512)
                        for jl in range(b - a):
                            jj = a + jl
                            for s in range(NT):
                                mm(PS[acc[s]].all(), hid[:, jj, s * 128:(s + 1) * 128],
                                   View(wv.ap[:, jl, :], wv.arena, wv.lo, wv.hi), jj == 0, jj == cfg.NJH - 1)
                        wdone()
                    for s in range(NT):
                        stt("dve", xt[:, s, n * 512:(n + 1) * 512], PS[acc[s]].all(), 0.5,
                            xt[:, s, n * 512:(n + 1) * 512], ALU.mult, ALU.add)
                        if half == 1:
                            sq_block(s, n)

        def proj_feat(NT, wname, evac):
            N = NT * 128
            w = wget(wname)
            for hh in range(2):
                wv = sub3(w, hh * 2048, 16, 128)
                b = nb_()
                for kc in range(KC):
                    mm(PS[b][:, 0:N], View(wv.ap[:, kc, :], wv.arena, wv.lo, wv.hi), hT[:, kc, 0:N], kc == 0, kc == KC - 1)
                evac(hh, b)
            wdone()

        def proj_tok(NT, wn, n, evac):
            acc = nb_(4)
            for half in range(2):
                w = wget((wn, n, half))
                wv = sub3(w, 0, 8, 512)
                for kcl in range(8):
                    kc = half * 8 + kcl
                    for s in range(NT):
                        mm(PS[acc[s]].all(), hT[:, kc, s * 128:(s + 1) * 128], View(wv.ap[:, kcl, :], wv.arena, wv.lo, wv.hi),
                           kc == 0, kc == KC - 1)
                wdone()
            for s in range(NT):
                evac(s, acc[s])

        def out_rows(dst_t, rows, col0, width, src_view):
            pass

        def mixer(tile):
            kind = tile["kind"]
            NT = tile["NT"]
            N = NT * 128
            norm_T(NT, 1)
            inv = 1.0 / math.sqrt(DH)
            for hp in range(4):
                def ev(hh, b, hp=hp):
                    S.op("act", lambda e: e.mul(qT[:, 2 * hp + hh, 0:N].ap, PS[b][:, 0:N].ap, inv), [PS[b][:, 0:N]], [qT[:, 2 * hp + hh, 0:N]])
                proj_feat(NT, ("Q", hp), ev)
            chk("mix_q")
            def rows_of(s):
                if kind == "P":
                    return [(tile["seq"], tile["pos0"] + s * 128, 0, 128)]
                return [(q, 0, q * cfg.TD, cfg.TD) for q in range(cfg.nseq_s)]

            def store(dname, s, col0, width, ti):
                if getattr(cfg, "nostore", False):
                    return
                for (sq, t0, p0, npp) in rows_of(s):
                    S.dma("pool", "o%d" % ti, A[dname][sq, t0:t0 + npp, col0:col0 + width], tmp[ti][p0:p0 + npp, 0:width])

            for n in range(2):
                def ev(s, b, n=n):
                    kd = getattr(cfg, "kdbg", 0)
                    if kd == 1:
                        return
                    ti = ntmp()
                    if kd != 3:
                        cp("act", tmp[ti].all(), PS[b].all())
                    store("kp" if kind == "P" else "ks", s, n * 512, 512, ti)
                    if kd != 2:
                        cp("dve", ktok[:, s, n * 512:(n + 1) * 512], tmp[ti].all())
                proj_tok(NT, "K", n, ev)
            chk("mix_k")
            for h in range(H):
                b = nb_()
                for s in range(NT):
                    tr(PSB[b][:, s * 128:(s + 1) * 128], ktok[:, s, h * 128:(h + 1) * 128], ident.all())
                if kind == "P":
                    cp("act" if h % 2 == 0 else "dve", KT[:, h, tile["pos0"]:tile["pos0"] + N], PSB[b][:, 0:N])
                else:
                    cp("act" if h % 2 == 0 else "dve", ktn[:, h, 0:N], PSB[b][:, 0:N])
            for n in range(2):
                def ev(s, b, n=n):
                    ti = ntmp()
                    cp("act", tmp[ti].all(), PS[b].all())
                    store("vp" if kind == "P" else "vs", s, n * 512, 512, ti)
                    if kind == "P":
                        cp("dve", Vc[:, tile["pos0"] // 128 + s, n * 512:(n + 1) * 512], tmp[ti].all())
                    else:
                        cp("dve", vnew[:, n * 512:(n + 1) * 512], tmp[ti].all())
                proj_tok(NT, "V", n, ev)
            chk("mix_v")
            w = wget(("F",))
            wv = sub3(w, 0, 16, 8)
            pts = []
            for s in range(NT):
                b = nb_()
                for kc in range(KC):
                    mm(PS[b][:, 0:H], hT[:, kc, s * 128:(s + 1) * 128], View(wv.ap[:, kc, :], wv.arena, wv.lo, wv.hi), kc == 0, kc == KC - 1)
                pt_ = (tile["pos0"] // 128 + s) if kind == "P" else NKT
                pts.append(pt_)
                tt("dve", fl[:, s, :], PS[b][:, 0:H], bfb.all(), ALU.add)
            act(fl[:, 0:NT, :], fl[:, 0:NT, :], AF.Sigmoid)
            for s in range(NT):
                pt_ = pts[s]
                act(logfc[:, pt_, :], fl[:, s, :], AF.Ln)
                for (sq, t0, p0, npp) in rows_of(s):
                    S.dma("pool", "lf", A["lp" if kind == "P" else "ls"][sq, t0:t0 + npp, :], logfc[p0:p0 + npp, pt_, :])

            def f_phase_b():
                for s in range(NT):
                    pt_ = pts[s]
                    b2 = nb_()
                    if kind == "P":
                        mm(PS[b2][:, 0:H], trif.all(), logfc[:, pt_, :], True, pt_ == 0)
                        if pt_ > 0:
                            mm(PS[b2][:, 0:H], onesf.all(), ltot.all(), False, True)
                        if pt_ == 0:
                            cp("pool", ltot.all(), logfc[:, pt_, :])
                        else:
                            tt("pool", ltot.all(), ltot.all(), logfc[:, pt_, :], ALU.add)
                    else:
                        mm(PS[b2][:, 0:H], triblk.all(), logfc[:, pt_, :], True, False)
                        for q in range(cfg.nseq_s):
                            mm(PS[b2][:, 0:H], halfsel[:, q, :], lptot[:, q, :], False, q == cfg.nseq_s - 1)
                    cp("act", ccol.all(), PS[b2][:, 0:H])
                    ts("pool", negc[:, pt_, :], ccol.all(), -1.0, ALU.mult)
                    b3 = nb_()
                    tr(PS[b3][0:H, 0:128], ccol.all(), identf.all())
                    c0 = (tile["pos0"] + s * 128) if kind == "P" else KTW
                    cp("act", cTb[0:H, c0:c0 + 128], PS[b3][0:H, 0:128])
            wdone()
            chk("mix_f")
            if kind == "P":
                segs = [(0, 0, N)]
            else:
                segs = [(q * (HIST + cfg.TD), q * cfg.TD, cfg.TD) for q in range(cfg.nseq_s)]
            UW = segs[-1][0] + HIST + segs[-1][2]
            for up in range(4):
                def ev(uu, b, up=up):
                    cu = 2 * up + uu
                    g = cu // 2
                    wdw = WINS[g]
                    u_ = ub[0]
                    for qi, (hc, tc, ntk) in enumerate(segs):
                        if kind == "P":
                            cp("pool", u_[:, hc:hc + HIST], hist[:, cu, :])
                        else:
                            cp("pool", u_[:, hc:hc + HIST], histS[qi][:, cu, :])
                        cp("act", u_[:, hc + HIST:hc + HIST + ntk], PS[b][:, tc:tc + ntk])
                    for qi, (hc, tc, ntk) in enumerate(segs):
                        if kind == "P":
                            cp("pool", hist[:, cu, :], u_[:, hc + ntk:hc + ntk + HIST])
                        else:
                            cp("pool", histS[qi][:, cu, :], u_[:, hc + ntk:hc + ntk + HIST])
                    cur = u_
                    sh = 1
                    k = 0
                    while sh < wdw:
                        nxtb = pa[k % 2]
                        tt("dve", nxtb[:, sh:UW], cur[:, sh:UW], cur[:, 0:UW - sh], ALU.add)
                        cur = nxtb
                        sh *= 2
                        k += 1
                    for (hc, tc, ntk) in segs:
                        if kind == "P" and tile["pos0"] == 0:
                            tt("dve", cur[:, HIST:HIST + 16], cur[:, HIST:HIST + 16], fixw[:, g, :], ALU.mult)
                        stt("dve", dT[:, cu, tc:tc + ntk], cur[:, hc + HIST:hc + HIST + ntk], 1.0 / wdw,
                            u_[:, hc + HIST:hc + HIST + ntk], ALU.mult, ALU.subtract)
                proj_feat(NT, ("U", up), ev)
            def pool_state_out():
                if kind == "S" or tile["last"]:
                    for qi in range(cfg.nseq_s if kind == "S" else 1):
                        hsrc = histS[qi] if kind == "S" else hist
                        b1, b2 = nb_(2)
                        for cu in range(8):
                            bb = b1 if cu < 4 else b2
                            tr(PS[bb][0:HIST, (cu % 4) * 128:(cu % 4 + 1) * 128], hsrc[:, cu, :], identf.all())
                        for hi_, bb in enumerate((b1, b2)):
                            ti = ntmp()
                            cp("dve", tmp[ti][0:HIST, :], PS[bb][0:HIST, :])
                            dst = A["pso"][qi] if kind == "S" else A["pp"][tile["seq"]]
                            S.dma("pool", "o%d" % ti, dst[:, hi_ * 512:(hi_ + 1) * 512], tmp[ti][0:HIST, :])

            f_phase_b()
            chk("mix_pso")
            if kind == "P":
                attention_prompt(tile)
            else:
                for q in range(cfg.nseq_s):
                    attention_sample(q)
            pool_state_out()
            chk("mix_attn")
            if getattr(cfg, "debug", False) and kind == "P" and tile["pos0"] == 0 and tile["seq"] == 0:
                S.dma("pool", "lf", A["dbg_qT"], qT.all())
                S.dma("pool", "lf", A["dbg_oT"], oT.all())
                S.dma("pool", "lf", A["dbg_dT"], dT.all())
                S.dma("pool", "lf", A["dbg_negc"], negc[:, 0:4, :])
                S.dma("pool", "lf", A["dbg_cTb"], cTb[0:8, 0:512])
                S.dma("pool", "lf", A["dbg_KT"].rearrange("p (h t) -> p h t", h=8), KT[:, :, 0:512])
                S.dma("pool", "lf", A["dbg_V"], Vc[:, 0:4, :])
            for c in range(16):
                g = c // 4
                wab = wget(("AB", c))
                wa = sub3(wab, 0, 8, 128)
                wpp = sub3(wab, 1024, 2, 128)
                wgt = wget(("G", c))
                wga = sub3(wgt, 0, 16, 128)
                wgb = sub3(wgt, 2048, 16, 128)
                bA, bB, bGa, bGb = nb_(4)
                for kc in range(8):
                    mm(PS[bA][:, 0:N], View(wa.ap[:, kc, :], wa.arena, wa.lo, wa.hi), oT[:, kc, 0:N], kc == 0, kc == 7)
                for cc in range(2):
                    mm(PS[bB][:, 0:N], View(wpp.ap[:, cc, :], wpp.arena, wpp.lo, wpp.hi), dT[:, 2 * g + cc, 0:N], cc == 0, cc == 1)
                for kc in range(KC):
                    mm(PS[bGa][:, 0:N], View(wga.ap[:, kc, :], wga.arena, wga.lo, wga.hi), hT[:, kc, 0:N], kc == 0, kc == KC - 1)
                for kc in range(KC):
                    mm(PS[bGb][:, 0:N], View(wgb.ap[:, kc, :], wgb.arena, wgb.lo, wgb.hi), hT[:, kc, 0:N], kc == 0, kc == KC - 1)
                wdone(2)
                t1, t2 = ntmp(), ntmp()
                act(tmp[t1][:, 0:N], PS[bGa][:, 0:N], AF.Sigmoid)
                act(tmp[t2][:, 0:N], PS[bGb][:, 0:N], AF.Sigmoid)
                tt("dve", tmp[t1][:, 0:N], PS[bA][:, 0:N], tmp[t1][:, 0:N], ALU.mult)
                stt("dve", tmp[t2][:, 0:N], PS[bB][:, 0:N], pscT[:, c:c + 1], tmp[t2][:, 0:N], ALU.mult, ALU.mult)
                tt("dve", mT[:, c, 0:N], tmp[t1][:, 0:N], tmp[t2][:, 0:N], ALU.add)
            if getattr(cfg, "debug", False) and kind == "P" and tile["pos0"] == 0 and tile["seq"] == 0:
                S.dma("pool", "lf", A["dbg_mT"], mT.all())
            for n in range(4):
                def ev(s, b, n=n):
                    tt("dve", xt[:, s, n * 512:(n + 1) * 512], PS[b].all(), xt[:, s, n * 512:(n + 1) * 512], ALU.add)
                    sq_block(s, n)
                acc = nb_(4)
                for half in range(2):
                    w = wget(("WO", n, half))
                    wv = sub3(w, 0, 8, 512)
                    for kcl in range(8):
                        kc = half * 8 + kcl
                        for s in range(NT):
                            mm(PS[acc[s]].all(), mT[:, kc, s * 128:(s + 1) * 128], View(wv.ap[:, kcl, :], wv.arena, wv.lo, wv.hi),
                               kc == 0, kc == KC - 1)
                    wdone()
                for s in range(NT):
                    ev(s, acc[s])

        def attn_core(h, chunks, qcol0, nq, out_col0):
            pass

        SKEW = 2

        def attention_prompt(tile):
            N = 512
            i4 = tile["pos0"] // 128
            nch = i4 + 4
            pos0 = tile["pos0"]
            prevb = ()
            for h in range(H):
                bO, bD = nb_(2)
                while bO in prevb or bD in prevb:
                    bO, bD = nb_(2)
                pbuf = {}
                excl_early = (bO, bD) + tuple(prevb)
                prevb = (bO, bD)

                def st1(j, h=h, bO=bO, bD=bD, pbuf=pbuf, excl_early=excl_early):
                    q0 = max(0, j - i4) * 128
                    bS = nb_excl(excl_early if j < 3 else (bO, bD))
                    mm(PS[bS][:, q0:N], KT[:, h, j * 128:(j + 1) * 128], qT[:, h, q0:N], True, False)
                    mm(PS[bS][:, q0:N], sel[:, h, :], cTb[:, pos0 + q0:pos0 + N], False, True)
                    p_ = PT[nxt("pt", NPT)]
                    act(p_[:, q0:N], PS[bS][:, q0:N], AF.Exp, bias=negc[:, j, h:h + 1])
                    if j >= i4:
                        tt("dve", p_[:, q0:q0 + 128], p_[:, q0:q0 + 128], trib.all(), ALU.mult)
                    pbuf[j] = (p_, q0)

                def st2(j, h=h, bO=bO, bD=bD, pbuf=pbuf):
                    p_, q0 = pbuf.pop(j)
                    mm(PS[bO][:, q0:N], Vc[:, j, h * 128:(h + 1) * 128], p_[:, q0:N], j == 0, j == nch - 1)
                    mm(PS[bD][:, q0:N], onesb.all(), p_[:, q0:N], j == 0, j == nch - 1)

                for j in range(nch + SKEW):
                    if j < nch:
                        st1(j)
                    if j - SKEW >= 0:
                        st2(j - SKEW)
                ti = ntmp()
                act(tmp[ti].all(), PS[bD].all(), AF.Ln)
                act(tmp[ti].all(), tmp[ti].all(), AF.Exp, scale=-1.0)
                tt("dve", oT[:, h, 0:N], PS[bO].all(), tmp[ti].all(), ALU.mult)

        histS = [S.alloc("sb", [128, 8, HIST], F32, at=21504 + 512 * i) for i in range(cfg.nseq_s)]

        def sample_prep():
            npast = cfg.PAST // 128
            for q in range(cfg.nseq_s):
                S.dma("sp", "c0", hst[0:HIST, :], A["spool"][q])
                for cu in range(8):
                    b = nb_()
                    tr(PS[b][:, 0:HIST], hst[0:HIST, cu * 128:(cu + 1) * 128], identf[0:HIST, 0:HIST])
                    cp("dve", histS[q][:, cu, :], PS[b][:, 0:HIST])
                S.dma("sp", "c1", lpast[:, q, 0:npast, :], A["cl"][q].rearrange("(j p) h -> p j h", p=128))
                S.op("dve", (lambda q=q: (lambda e: e.tensor_reduce(lptot[:, q, :].ap, lpast[:, q, 0:npast, :].ap.rearrange("p j h -> p h j"),
                                                                   mybir.AxisListType.X, ALU.add)))(),
                     [lpast[:, q, 0:npast, :]], [lptot[:, q, :]])

        def attention_sample(q):
            TD = cfg.TD
            npast = cfg.PAST // 128
            for jt in range(npast):
                b = nb_()
                mm(PS[b][:, 0:H], trif.all(), lpast[:, q, jt, :], True, jt == 0)
                if jt > 0:
                    mm(PS[b][:, 0:H], onesf.all(), lrun.all(), False, True)
                ts("dve", negc[:, jt, :], PS[b][:, 0:H], -1.0, ALU.mult)
                if jt == 0:
                    cp("dve", lrun.all(), lpast[:, q, jt, :])
                elif jt < npast - 1:
                    tt("dve", lrun.all(), lrun.all(), lpast[:, q, jt, :], ALU.add)
            S.dma("sp", "sh0", negS[0:TD, :], negc[q * TD:(q + 1) * TD, NKT, :])
            S.dma("sp", "sh1", Vc[0:TD, NKT, :], vnew[q * TD:(q + 1) * TD, :])
            for jt in range(npast):
                ks_ = cst[nxt("cst", 2)]
                S.dma("sp", "c%d" % rot["cst"], ks_.all(), A["ck"][q, jt * 128:(jt + 1) * 128, :])
                kb = cbf[nxt("cbf", 2)]
                cp("act", kb.all(), ks_.all())
                for hq in range(2):
                    b = nb_()
                    for hh in range(4):
                        h = hq * 4 + hh
                        tr(PSB[b][:, hh * 128:(hh + 1) * 128], kb[:, h * 128:(h + 1) * 128], ident.all())
                    src = View(PSB[b][:, 0:512].ap.rearrange("p (a b) -> p a b", b=128), PSB[b].arena, 0, 1024)
                    cp("dve" if hq == 0 else "act", KT[:, hq * 4:(hq + 1) * 4, jt * 128:(jt + 1) * 128], src)
                vs_ = cst[nxt("cst", 2)]
                S.dma("sp", "c%d" % rot["cst"], vs_.all(), A["cv"][q, jt * 128:(jt + 1) * 128, :])
                cp("dve" if jt % 2 == 0 else "act", Vc[:, jt, :], vs_.all())
            qc = q * TD
            for h in range(H):
                bO, bD = nb_(2)
                for j in range(npast + 1):
                    bS = nb_excl((bO, bD))
                    p_ = PT[nxt("pt", NPT)]
                    if j < npast:
                        nk = 128
                        mm(PS[bS][:, 0:TD], KT[:, h, j * 128:(j + 1) * 128], qT[:, h, qc:qc + TD], True, False)
                        mm(PS[bS][:, 0:TD], sel[:, h, :], cTb[:, KTW + qc:KTW + qc + TD], False, True)
                        act(p_[:, 0:TD], PS[bS][:, 0:TD], AF.Exp, bias=negc[:, j, h:h + 1])
                        mm(PS[bO][:, 0:TD], Vc[:, j, h * 128:(h + 1) * 128], p_[:, 0:TD], j == 0, False)
                        mm(PS[bD][:, 0:TD], onesb.all(), p_[:, 0:TD], j == 0, False)
                    else:
                        mm(PS[bS][0:TD, 0:TD], ktn[:, h, qc:qc + TD], qT[:, h, qc:qc + TD], True, False)
                        mm(PS[bS][0:TD, 0:TD], sel[:, h, 0:TD], cTb[:, KTW + qc:KTW + qc + TD], False, True)
                        act(p_[0:TD, 0:TD], PS[bS][0:TD, 0:TD], AF.Exp, bias=negS[0:TD, h:h + 1])
                        tt("dve", p_[0:TD, 0:TD], p_[0:TD, 0:TD], trib[0:TD, 0:TD], ALU.mult)
                        mm(PS[bO][:, 0:TD], Vc[0:TD, NKT, h * 128:(h + 1) * 128], p_[0:TD, 0:TD], False, True)
                        mm(PS[bD][:, 0:TD], onesb[0:TD, :], p_[0:TD, 0:TD], False, True)
                ti = ntmp()
                act(tmp[ti][:, 0:TD], PS[bD][:, 0:TD], AF.Ln)
                act(tmp[ti][:, 0:TD], tmp[ti][:, 0:TD], AF.Exp, scale=-1.0)
                tt("dve", oT[:, h, qc:qc + TD], PS[bO][:, 0:TD], tmp[ti][:, 0:TD], ALU.mult)

        negS = S.alloc("sb", [128, H], F32, at=22528)
        lrun = S.alloc("sb", [128, H], F32, at=22528 + 64)
        print("sbuf used (final)", S.cursor("sb"), "of", ARENA)

        ystage = [S.alloc("sb", [128, 512], F32, at=hT.off + i * 2048) for i in range(8)]

        def load_x(tile, s):
            if tile["kind"] == "P":
                S.dma("sp", "x%d" % s, xt[:, s, :], A["xp"][tile["seq"], tile["pos0"] + s * 128:tile["pos0"] + (s + 1) * 128, :])
            elif s == 0:
                for q in range(cfg.nseq_s):
                    S.dma("sp", "x%d" % q, xt[q * cfg.TD:(q + 1) * cfg.TD, 0, :], A["xs"][q])

        def final_out(tile, nxt_tile=None):
            NT = tile["NT"]
            kind = tile["kind"]
            rstd_from_parts(NT)
            w = wget("GF")
            v = w.all()
            gf = View(v.ap.bitcast(F32), v.arena, v.lo, v.hi)
            for s in range(NT):
                for n in range(4):
                    gfn = View(gf.ap[:, n * 512:(n + 1) * 512], gf.arena, gf.lo, gf.hi)
                    if n % 2 == 1:
                        yi = (s * 2 + n // 2) % 8
                        stg = ystage[yi]
                        key = "y%d" % yi
                    else:
                        ti = ntmp()
                        stg = tmp[ti]
                        key = "o%d" % ti
                    stt("dve", stg.all(), xt[:, s, n * 512:(n + 1) * 512], rstd[:, s:s + 1], gfn, ALU.mult, ALU.mult)
                    if kind == "P":
                        S.dma("pool", key, A["yp"][tile["seq"], tile["pos0"] + s * 128:tile["pos0"] + (s + 1) * 128, n * 512:(n + 1) * 512], stg.all())
                    else:
                        for q in range(cfg.nseq_s):
                            S.dma("pool", key, A["ys"][q, :, n * 512:(n + 1) * 512], stg[q * cfg.TD:(q + 1) * cfg.TD, :])
                if nxt_tile is not None and s < nxt_tile["NT"]:
                    load_x(nxt_tile, s)
            wdone()

        def run_tile(tile, nxt_tile=None, first=False):
            NT = tile["NT"]
            kind = tile["kind"]
            if kind == "P" and tile["pos0"] == 0:
                memset("pool", hist.all(), 0.0)
            if first:
                for s in range(NT):
                    load_x(tile, s)
            if kind == "S":
                sample_prep()
            for s in range(NT):
                for n in range(4):
                    if s % 2 == 1 and getattr(cfg, "dve_sq", True):
                        blk = xt[:, s, n * 512:(n + 1) * 512]
                        jb = junks[nxt("junk", 4)].all()
                        S.op("dve", (lambda blk=blk, jb=jb, s=s, n=n: (lambda e: e.scalar_tensor_tensor(
                            jb.ap, blk.ap, 1.0, blk.ap, ALU.mult, ALU.mult, accum_out=ssqp[:, s, n:n + 1].ap)))(),
                            [blk], [jb, ssqp[:, s, n:n + 1]])
                    else:
                        sq_block(s, n)
            chk("xload")
            ffn(NT, 1, 0)
            chk("ffn1")
            mixer(tile)
            chk("mixer")
            ffn(NT, 2, 2)
            chk("ffn2")
            final_out(tile, nxt_tile)
            chk("tile0")
            if kind == "P" and tile["last"] and tile["seq"] == cfg.nseq_p - 1:
                chk("ptiles")

        tiles = []
        for b in range(cfg.nseq_p):
            for i in range(cfg.S // 512):
                tiles.append({"kind": "P", "NT": 4, "seq": b, "pos0": i * 512, "last": i == cfg.S // 512 - 1})
        tiles.append({"kind": "S", "NT": 1})
        try:
            if stop_at not in ("consts",):
                for it_, t in enumerate(tiles):
                    run_tile(t, tiles[it_ + 1] if it_ + 1 < len(tiles) else None, first=(it_ == 0))
            if stop_at is None:
                assert wstate["next"] == TOTAL and wstate["done"] == TOTAL, (wstate, TOTAL)
        except StopBuild as e_:
            print("STOPPED AT", e_)

        S.prepare()
        okeys = [k for k in dkeys if k.startswith("o") or k in ("lf",) or k.startswith("ps") or k.startswith("y")]
        stats = {}

        @block.tensor
        def _(e):
            stats["pe"] = S.emit_engine("pe", e, sem_e, sem_d)

        @block.scalar
        def _(e):
            stats["act"] = S.emit_engine("act", e, sem_e, sem_d)

        @block.vector
        def _(e):
            stats["dve"] = S.emit_engine("dve", e, sem_e, sem_d)

        @block.gpsimd
        def _(e):
            stats["pool"] = S.emit_engine("pool", e, sem_e, sem_d, final_keys=okeys)

        @block.sync
        def _(e):
            stats["sp"] = S.emit_engine("sp", e, sem_e, sem_d, final_keys=[k for k in dkeys if k not in okeys])
        print("ops/waits", stats)
    return nc


def make_in_maps(cfg, inp, ncores):
    maps = []
    g = lambda k: np.ascontiguousarray(np.asarray(inp[k], dtype=np.float32))
    w = {}
    w["n1"] = g("ffn1_norm")[0]
    w["wg1"] = g("ffn1_w_gate")[0]
    w["wu1"] = g("ffn1_w_up")[0]
    w["wd1"] = g("ffn1_w_down")[0]
    w["n2"] = g("ffn2_norm")[0]
    w["wg2"] = g("ffn2_w_gate")[0]
    w["wu2"] = g("ffn2_w_up")[0]
    w["wd2"] = g("ffn2_w_down")[0]
    w["nm"] = g("mix_norm")[0]
    w["win"] = g("w_in")[0]
    w["bf"] = g("b_forget")[0]
    w["wba"] = g("w_branch_attn")[0]
    wp = g("w_pool_group")[0]
    for gi in range(4):
        w["wp%d" % gi] = np.ascontiguousarray(wp[gi])
    w["psc"] = g("pool_scale")[0]
    w["wo"] = g("w_out")[0]
    w["nf"] = g("final_norm")
    xp = g("x_prompt")
    xs = g("x_sample")
    ck = g("cache_k")[0].reshape(xs.shape[0], -1, AW)
    cv = g("cache_v")[0].reshape(xs.shape[0], -1, AW)
    cl = g("cache_logf")[0]
    sp = g("state_pool")[0]
    for c in range(ncores):
        m = dict(w)
        m["xp"] = np.ascontiguousarray(xp[c * cfg.nseq_p:(c + 1) * cfg.nseq_p])
        sl = slice(c * cfg.nseq_s, (c + 1) * cfg.nseq_s)
        m["xs"] = np.ascontiguousarray(xs[sl])
        m["ck"] = np.ascontiguousarray(ck[sl])
        m["cv"] = np.ascontiguousarray(cv[sl])
        m["cl"] = np.ascontiguousarray(cl[sl])
        m["spool"] = np.ascontiguousarray(sp[sl])
        maps.append(m)
    return maps


def gather(cfg, results):
    cat = lambda k: np.concatenate([r[k] for r in results], axis=0)
    yp = cat("yp")
    ys = cat("ys")
    kp = cat("kp").reshape(1, -1, cfg.S, H, DH)
    vp = cat("vp").reshape(1, -1, cfg.S, H, DH)
    lp = cat("lp")[None]
    pp = cat("pp")[None]
    ks = cat("ks").reshape(1, -1, cfg.TD, H, DH)
    vs = cat("vs").reshape(1, -1, cfg.TD, H, DH)
    ls = cat("ls")[None]
    pso = cat("pso")[None]
    return (yp, ys, kp, vp, lp, pp, ks, vs, ls, pso)


_CACHE = {}


def kernel(**inputs):
    cfg = Cfg()
    ncores = 8
    if "nc" not in _CACHE:
        _CACHE["nc"] = build_program(cfg)
    nc = _CACHE["nc"]
    maps = make_in_maps(cfg, inputs, ncores)
    res = run_bass_kernel_spmd(nc, maps, core_ids=list(range(ncores)))
    return gather(cfg, res.results)
```

```python
import bisect
import numpy as np
import concourse.bass as bass
import concourse.mybir as mybir

F32 = mybir.dt.float32
BF16 = mybir.dt.bfloat16
U8 = mybir.dt.uint8
AF = mybir.ActivationFunctionType
ALU = mybir.AluOpType
DTSIZE = {F32: 4, BF16: 2, U8: 1}

ENGS = ("pe", "act", "dve", "pool", "sp")


class Op:
    __slots__ = ("eng", "fn", "waits", "sig", "sigval", "dma_key", "idx", "needs_sig")

    def __init__(self, eng, fn, dma_key=None):
        self.eng = eng
        self.fn = fn
        self.waits = {}
        self.needs_sig = False
        self.sigval = None
        self.dma_key = dma_key
        self.idx = -1


class IntervalMap:
    def __init__(self, size):
        self.starts = [0]
        self.segs = {0: [size, None, {}]}

    def _split(self, pos):
        i = bisect.bisect_right(self.starts, pos) - 1
        s = self.starts[i]
        seg = self.segs[s]
        if s == pos or pos >= seg[0]:
            return
        end = seg[0]
        seg[0] = pos
        self.segs[pos] = [end, seg[1], dict(seg[2])]
        self.starts.insert(i + 1, pos)

    def access(self, lo, hi, op, write):
        self._split(lo)
        self._split(hi)
        i = bisect.bisect_left(self.starts, lo)
        deps = []
        j = i
        while j < len(self.starts) and self.starts[j] < hi:
            seg = self.segs[self.starts[j]]
            if seg[1] is not None:
                deps.append(seg[1])
            if write:
                deps.extend(seg[2].values())
            j += 1
        if write:
            for k in range(i + 1, j):
                del self.segs[self.starts[k]]
            del self.starts[i + 1:j]
            self.segs[lo] = [hi, op, {}]
        else:
            rk = (op.eng, op.dma_key)
            for k in range(i, j):
                self.segs[self.starts[k]][2][rk] = op
        return deps


class View:
    __slots__ = ("ap", "arena", "lo", "hi")

    def __init__(self, ap, arena, lo, hi):
        self.ap = ap
        self.arena = arena
        self.lo = lo
        self.hi = hi


class Buf:
    def __init__(self, sched, arena, off, shape, dtype, ap):
        self.s = sched
        self.arena = arena
        self.off = off
        self.shape = tuple(shape)
        self.dtype = dtype
        self.ap = ap
        self.esz = DTSIZE[dtype]
        st = [1]
        for d in reversed(self.shape[2:]):
            st.insert(0, st[0] * d)
        self.strides = st
        self.nbytes = int(np.prod(self.shape[1:])) * self.esz

    def __getitem__(self, key):
        if not isinstance(key, tuple):
            key = (key,)
        ap = self.ap[key]
        fk = list(key[1:]) + [slice(None)] * (len(self.shape) - len(key))
        lo = 0
        hi = 0
        for k, dim, st in zip(fk, self.shape[1:], self.strides):
            if isinstance(k, slice):
                a, b, step = k.indices(dim)
                assert step == 1
                lo += a * st
                hi += (b - 1) * st
            else:
                lo += k * st
                hi += k * st
        return View(ap, self.arena, self.off + lo * self.esz, self.off + (hi + 1) * self.esz)

    def all(self):
        return self[tuple(slice(None) for _ in self.shape)]


class Sched:
    def __init__(self, nc):
        self.nc = nc
        self.ops = {e: [] for e in ENGS}
        self.maps = {}
        self.nops = 0
        self.dma_counts = {}
        self.dma_last = {}
        self.arenas = {}
        self.same_eng_sync = ("act", "dve", "pool")
        self.relax_same_eng_waw = False

    def add_arena(self, name, size):
        self.maps[name] = IntervalMap(size)

    def add_sbuf_arena(self, name, handle, size):
        self.arenas[name] = [handle, size, 0]
        self.add_arena(name, size)

    def alloc(self, arena, shape, dtype, at=None, align=64):
        h, size, cur = self.arenas[arena]
        esz = DTSIZE[dtype]
        nb = int(np.prod(shape[1:])) * esz
        if at is None:
            at = (cur + align - 1) // align * align
            self.arenas[arena][2] = at + nb
        assert at + nb <= size, (arena, at, nb, size)
        ap = h[0:shape[0], at:at + nb]
        if dtype != U8:
            ap = ap.bitcast(dtype)
        if len(shape) > 2:
            names = " ".join("d%d" % i for i in range(len(shape) - 1))
            kw = {"d%d" % i: shape[i + 1] for i in range(1, len(shape) - 1)}
            ap = ap.rearrange("p (%s) -> p %s" % (names, names), **kw)
        return Buf(self, arena, at, shape, dtype, ap)

    def cursor(self, arena):
        return self.arenas[arena][2]

    def set_cursor(self, arena, v):
        self.arenas[arena][2] = v

    def _add(self, op, reads, writes):
        op.idx = self.nops
        self.nops += 1
        rdeps = []
        wdeps = []
        for v in reads:
            rdeps.extend(self.maps[v.arena].access(v.lo, v.hi, op, False))
        for v in writes:
            wdeps.extend(self.maps[v.arena].access(v.lo, v.hi, op, True))
        for israw, deps in ((True, rdeps), (False, wdeps)):
            for d in deps:
                if d is op:
                    continue
                if d.eng == op.eng and d.dma_key is None and op.dma_key is None:
                    if op.eng not in self.same_eng_sync:
                        continue
                    if (not israw) and self.relax_same_eng_waw:
                        continue
                d.needs_sig = True
                op.waits[id(d)] = d
        self.ops[op.eng].append(op)
        return op

    def op(self, eng, fn, reads=(), writes=()):
        return self._add(Op(eng, fn), reads, writes)

    def dma(self, eng, key, out, in_, reads=(), writes=(), **kw):
        oap = out.ap if isinstance(out, View) else out
        iap = in_.ap if isinstance(in_, View) else in_
        rd = list(reads) + ([in_] if isinstance(in_, View) else [])
        wr = list(writes) + ([out] if isinstance(out, View) else [])

        def fn(e):
            return e.dma_start(out=oap, in_=iap, **kw)
        op = Op(eng, fn, dma_key=key)
        prev = self.dma_last.get(key)
        self._add(op, rd, wr)
        if prev is not None:
            op.waits[id(prev)] = prev
        self.dma_last[key] = op
        op.needs_sig = True
        return op

    def prepare(self):
        for e in ENGS:
            cnt = 0
            for op in self.ops[e]:
                if op.dma_key is not None:
                    c = self.dma_counts.get(op.dma_key, 0) + 1
                    self.dma_counts[op.dma_key] = c
                    op.sigval = ("d", op.dma_key, 16 * c)
                elif op.needs_sig:
                    cnt += 1
                    op.sigval = ("e", e, cnt)

    def emit_engine(self, e, eng, sems, dma_sems, final_keys=None):
        seen = {}
        nw = 0
        for op in self.ops[e]:
            need = {}
            for d in op.waits.values():
                kind, k, v = d.sigval
                kk = (kind, k)
                if seen.get(kk, 0) >= v:
                    continue
                if need.get(kk, 0) < v:
                    need[kk] = v
            for kk, v in need.items():
                sem = sems[kk[1]] if kk[0] == "e" else dma_sems[kk[1]]
                eng.wait_ge(sem, v)
                seen[kk] = v
                nw += 1
            ins = op.fn(eng)
            if op.dma_key is not None:
                ins.then_inc(dma_sems[op.dma_key], 16)
            elif op.needs_sig:
                ins.then_inc(sems[e], 1)
        if final_keys is not None:
            for k in final_keys:
                c = self.dma_counts.get(k, 0)
                if c:
                    eng.wait_ge(dma_sems[k], 16 * c)
        return (len(self.ops[e]), nw)
import math
from concourse.bass_utils import run_bass_kernel_spmd

D = 2048
KC = 16
H = 8
DH = 128
AW = 1024
PW = 1024
INW = 8200
C_Q, C_K, C_V, C_F, C_U, C_GA, C_GB = 0, 1024, 2048, 3072, 3080, 4104, 6152
WINS = (2, 4, 8, 16)
HIST = 15
EPS = 1e-6
SLOT = 4096
NSLOT = 3
NTMP = 6
ARENA = 211968


class Cfg:
    def __init__(self, nseq_p=2, S_=2048, nseq_s=2, TD=64, PAST=2048, DFF=5632):
        self.nseq_p, self.S, self.nseq_s, self.TD, self.PAST, self.DFF = nseq_p, S_, nseq_s, TD, PAST, DFF
        self.NJ = DFF // 128
        self.NJH = self.NJ // 2
        assert self.NJ % 2 == 0 and S_ % 512 == 0 and PAST % 128 == 0 and nseq_s * TD == 128


def jgroups(n):
    g = []
    a = 0
    while a < n:
        b = min(n, a + 8)
        g.append((a, b))
        a = b
    return g


def block_list(cfg):
    bl = []
    for f in (1, 2):
        for half in range(2):
            for jj in range(cfg.NJH):
                j = half * cfg.NJH + jj
                bl.append((("GU", f, j), [(0, "wg%d" % f, 0, 16, j * 128, 128), (2048, "wu%d" % f, 0, 16, j * 128, 128)]))
            for n in range(4):
                for (a, b) in jgroups(cfg.NJH):
                    bl.append((("D", f, half, n, a), [(0, "wd%d" % f, half * cfg.NJH + a, b - a, n * 512, 512)]))
        if f == 1:
            for hp in range(4):
                bl.append((("Q", hp), [(0, "win", 0, 16, C_Q + hp * 256, 128), (2048, "win", 0, 16, C_Q + hp * 256 + 128, 128)]))
            for n in range(2):
                for half in range(2):
                    bl.append((("K", n, half), [(0, "win", half * 8, 8, C_K + n * 512, 512)]))
            for n in range(2):
                for half in range(2):
                    bl.append((("V", n, half), [(0, "win", half * 8, 8, C_V + n * 512, 512)]))
            bl.append((("F",), [(0, "win", 0, 16, C_F, 8)]))
            for up in range(4):
                bl.append((("U", up), [(0, "win", 0, 16, C_U + up * 256, 128), (2048, "win", 0, 16, C_U + up * 256 + 128, 128)]))
            for c in range(16):
                g = c // 4
                bl.append((("AB", c), [(0, "wba", 0, 8, c * 128, 128), (1024, "wp%d" % g, 0, 2, (c % 4) * 128, 128)]))
                bl.append((("G", c), [(0, "win", 0, 16, C_GA + c * 128, 128), (2048, "win", 0, 16, C_GB + c * 128, 128)]))
            for n in range(4):
                for half in range(2):
                    bl.append((("WO", n, half), [(0, "wo", half * 8, 8, n * 512, 512)]))
    return bl


class StopBuild(Exception):
    pass


def build_program(cfg):
    nc = bass.Bass("TRN2", target_bir_lowering=False)
    stop_at = getattr(cfg, "stop", None)

    def chk(name):
        if stop_at == name:
            raise StopBuild(name)
    DFF = cfg.DFF

    def din(name, shape):
        return nc.dram_tensor(name, list(shape), F32, kind="ExternalInput")

    def dout(name, shape):
        return nc.dram_tensor(name, list(shape), F32, kind="ExternalOutput")

    T = {}
    T["xp"] = din("xp", [cfg.nseq_p, cfg.S, D])
    T["xs"] = din("xs", [cfg.nseq_s, cfg.TD, D])
    T["ck"] = din("ck", [cfg.nseq_s, cfg.PAST, AW])
    T["cv"] = din("cv", [cfg.nseq_s, cfg.PAST, AW])
    T["cl"] = din("cl", [cfg.nseq_s, cfg.PAST, H])
    T["spool"] = din("spool", [cfg.nseq_s, HIST, PW])
    for f in (1, 2):
        T["n%d" % f] = din("n%d" % f, [D])
        T["wg%d" % f] = din("wg%d" % f, [D, DFF])
        T["wu%d" % f] = din("wu%d" % f, [D, DFF])
        T["wd%d" % f] = din("wd%d" % f, [DFF, D])
    T["nm"] = din("nm", [D])
    T["win"] = din("win", [D, INW])
    T["bf"] = din("bf", [H])
    T["wba"] = din("wba", [AW, D])
    for g in range(4):
        T["wp%d" % g] = din("wp%d" % g, [256, 512])
    T["psc"] = din("psc", [D])
    T["wo"] = din("wo", [D, D])
    T["nf"] = din("nf", [D])
    T["yp"] = dout("yp", [cfg.nseq_p, cfg.S, D])
    T["ys"] = dout("ys", [cfg.nseq_s, cfg.TD, D])
    T["kp"] = dout("kp", [cfg.nseq_p, cfg.S, AW])
    T["vp"] = dout("vp", [cfg.nseq_p, cfg.S, AW])
    T["lp"] = dout("lp", [cfg.nseq_p, cfg.S, H])
    T["pp"] = dout("pp", [cfg.nseq_p, HIST, PW])
    T["ks"] = dout("ks", [cfg.nseq_s, cfg.TD, AW])
    T["vs"] = dout("vs", [cfg.nseq_s, cfg.TD, AW])
    T["ls"] = dout("ls", [cfg.nseq_s, cfg.TD, H])
    T["pso"] = dout("pso", [cfg.nseq_s, HIST, PW])
    if getattr(cfg, "debug", False):
        for nm_, shp, dt_ in (("dbg_qT", [128, 8 * 512], BF16), ("dbg_oT", [128, 8 * 512], BF16), ("dbg_dT", [128, 8 * 512], BF16),
                              ("dbg_mT", [128, 16 * 512], BF16), ("dbg_negc", [128, 4 * H], F32), ("dbg_cTb", [8, 512], BF16),
                              ("dbg_KT", [128, 8 * 512], BF16), ("dbg_V", [128, 4 * AW], BF16)):
            T[nm_] = nc.dram_tensor(nm_, shp, dt_, kind="ExternalOutput")
    A = {k: v.ap() for k, v in T.items()}

    blocks = block_list(cfg)
    NB = len(blocks)
    wsc_t = nc.dram_tensor("wsc", [NB, 128, SLOT], BF16, kind="Internal")
    wsc = wsc_t.ap()

    import contextlib
    es = contextlib.ExitStack()
    with es:
        arena = es.enter_context(nc.sbuf_tensor("arena", [128, ARENA], U8))
        psb = [es.enter_context(nc.psum_tensor("ps%d" % i, [128, 512], F32)) for i in range(8)]
        sem_e = {e: es.enter_context(nc.semaphore("s_" + e)) for e in ENGS}
        dkeys = (["w%d" % i for i in range(NSLOT)] + ["x%d" % i for i in range(4)] + ["o%d" % i for i in range(NTMP)]
                 + ["pl%d" % i for i in range(6)] + ["ps%d" % i for i in range(6)] + ["wb%d" % i for i in range(8)]
                 + ["c0", "c1", "c2", "c3", "misc", "sh0", "sh1", "lf"] + ["y%d" % i for i in range(8)])
        sem_d = {k: es.enter_context(nc.semaphore("d_" + k)) for k in dkeys}
        block = es.enter_context(nc.Block())

        S = Sched(nc)
        S.add_sbuf_arena("sb", arena, ARENA)
        S.add_arena("wsc", NB * SLOT)
        PS = []
        PSB = []
        for i in range(8):
            S.add_arena("ps%d" % i, 2048)
            PS.append(Buf(S, "ps%d" % i, 0, [128, 512], F32, psb[i][:, :]))
            PSB.append(Buf(S, "ps%d" % i, 0, [128, 1024], BF16, psb[i][:, :].bitcast(BF16)))
        bank_ctr = [0]

        def nb_(n=1):
            r = []
            for _ in range(n):
                r.append(bank_ctr[0] % 8)
                bank_ctr[0] += 1
            return r if n > 1 else r[0]

        def nb_excl(excl):
            while True:
                b = nb_()
                if b not in excl:
                    return b

        xt = S.alloc("sb", [128, 4, D], F32)
        hT = S.alloc("sb", [128, KC, 512], BF16)
        R0 = S.cursor("sb")
        xn = S.alloc("sb", [128, 4, D], BF16, at=R0)
        hid = S.alloc("sb", [128, 22, 512], BF16, at=R0)
        dT = S.alloc("sb", [128, 8, 512], BF16, at=R0)
        oT = S.alloc("sb", [128, 8, 512], BF16, at=R0 + 8192)
        ktok = S.alloc("sb", [128, 4, AW], BF16, at=R0 + 8192)
        qT = S.alloc("sb", [128, 8, 512], BF16, at=R0 + 16384)
        mT = S.alloc("sb", [128, 16, 512], BF16, at=R0 + 16384)
        S.set_cursor("sb", R0 + 32768)
        KTW = max(cfg.S, cfg.PAST, 2048)
        NKT = KTW // 128
        KT = S.alloc("sb", [128, H, KTW], BF16)
        Vc = S.alloc("sb", [128, NKT + 1, AW], BF16)
        wsl = [S.alloc("sb", [128, SLOT], BF16) for _ in range(NSLOT)]
        tmp = [S.alloc("sb", [128, 512], F32) for _ in range(NTMP)]
        ub = [S.alloc("sb", [128, 15 + 512], F32, at=R0 + 8192)]
        pa = [S.alloc("sb", [128, 15 + 512], F32, at=R0 + 8192 + 2112 * (i + 1)) for i in range(2)]
        hist = S.alloc("sb", [128, 8, HIST], F32)
        NPT = 4
        PT = [S.alloc("sb", [128, 512], BF16) for _ in range(NPT)]
        sg = [S.alloc("sb", [128, 512], BF16) for _ in range(2)]
        logfc = S.alloc("sb", [128, NKT + 1, H], F32)
        negc = S.alloc("sb", [128, NKT + 1, H], F32)
        ccol = S.alloc("sb", [128, H], F32)
        cTb = S.alloc("sb", [128, KTW + 128], BF16)
        ktn = S.alloc("sb", [128, H, 128], BF16, at=16384)
        vnew = S.alloc("sb", [128, AW], BF16, at=18432)
        lpast = S.alloc("sb", [128, 2, NKT, H], F32, at=20480)
        ident = S.alloc("sb", [128, 128], BF16)
        identf = S.alloc("sb", [128, 128], F32)
        onesb = S.alloc("sb", [128, 128], BF16)
        onesf = S.alloc("sb", [128, 128], F32)
        trif = S.alloc("sb", [128, 128], F32)
        trib = S.alloc("sb", [128, 128], BF16)
        triblk = S.alloc("sb", [128, 128], F32)
        halfsel = S.alloc("sb", [128, 2, 128], F32)
        sel = S.alloc("sb", [128, H, 128], BF16)
        self_f = S.alloc("sb", [128, H, 128], F32, at=tmp[0].off)
        gT = S.alloc("sb", [128, 3, KC], F32)
        pscT = S.alloc("sb", [128, KC], F32)
        bfb = S.alloc("sb", [128, H], F32)
        epsb = S.alloc("sb", [128, 1], F32)
        ssq = S.alloc("sb", [128, 4], F32)
        ssqp = S.alloc("sb", [128, 4, 4], F32)
        junks = [S.alloc("sb", [128, 512], BF16) for _ in range(4)]
        rstd = S.alloc("sb", [128, 4], F32)
        fixw = S.alloc("sb", [128, 4, 16], F32)
        fl = S.alloc("sb", [128, 4, H], F32)
        ltot = S.alloc("sb", [128, H], F32)
        lptot = S.alloc("sb", [128, 2, H], F32)
        print("sbuf used", S.cursor("sb"), "of", ARENA)
        NPS = 6
        PCE = 1024
        pbase = Vc.off + 4 * 2048
        assert NKT + 1 >= 16
        pst = [S.alloc("sb", [128, PCE], F32, at=pbase + i * 4096) for i in range(NPS)]
        cst = [S.alloc("sb", [128, AW], F32, at=8192 + i * 4096) for i in range(2)]
        cbf = [S.alloc("sb", [128, AW], BF16, at=8192 + 16384 + i * 2048) for i in range(2)]
        hst = S.alloc("sb", [16, PW], F32, at=8192 + 16384 + 4096)

        tmp_ctr = [0]

        def ntmp():
            i = tmp_ctr[0] % NTMP
            tmp_ctr[0] += 1
            return i

        rot = {}

        def nxt(name, n):
            rot[name] = (rot.get(name, -1) + 1) % n
            return rot[name]

        def mm(out, lhsT, rhs, start, stop):
            S.op("pe", lambda e: e.matmul(out.ap, lhsT.ap, rhs.ap, start=start, stop=stop), [lhsT, rhs], [out])

        def tr(out, in_, idn):
            S.op("pe", lambda e: e.transpose(out.ap, in_.ap, idn.ap), [in_, idn], [out])

        def act(out, in_, func, bias=None, scale=None, accum=None, extra_w=()):
            kw = {}
            rd = [in_]
            if bias is not None:
                kw["bias"] = bias.ap if isinstance(bias, View) else bias
                if isinstance(bias, View):
                    rd.append(bias)
            if scale is not None:
                kw["scale"] = scale.ap if isinstance(scale, View) else scale
                if isinstance(scale, View):
                    rd.append(scale)
            wr = [out]
            if accum is not None:
                kw["accum_out"] = accum.ap
                wr.append(accum)
            S.op("act", lambda e: e.activation(out.ap, in_.ap, func, **kw), rd, wr)

        def tt(eng, out, in0, in1, op):
            S.op(eng, lambda e: e.tensor_tensor(out.ap, in0.ap, in1.ap, op), [in0, in1], [out])

        def ts(eng, out, in0, s1, op0, s2=None, op1=None):
            rd = [in0]
            a1 = s1.ap if isinstance(s1, View) else s1
            if isinstance(s1, View):
                rd.append(s1)
            if op1 is None:
                S.op(eng, lambda e: e.tensor_scalar(out.ap, in0.ap, a1, None, op0), rd, [out])
            else:
                a2 = s2.ap if isinstance(s2, View) else s2
                if isinstance(s2, View):
                    rd.append(s2)
                S.op(eng, lambda e: e.tensor_scalar(out.ap, in0.ap, a1, a2, op0, op1), rd, [out])

        def stt(eng, out, in0, sc, in1, op0, op1):
            rd = [in0, in1]
            a = sc.ap if isinstance(sc, View) else sc
            if isinstance(sc, View):
                rd.append(sc)
            S.op(eng, lambda e: e.scalar_tensor_tensor(out.ap, in0.ap, a, in1.ap, op0, op1), rd, [out])

        def cp(eng, out, in_):
            if eng == "act":
                S.op("act", lambda e: e.copy(out.ap, in_.ap), [in_], [out])
            else:
                S.op(eng, lambda e: e.tensor_copy(out.ap, in_.ap), [in_], [out])

        def memset(eng, out, val):
            S.op(eng, lambda e: e.memset(out.ap, val), [], [out])

        def asel(out, pattern, cmp, fill, base, cm):
            S.op("pool", lambda e: e.affine_select(out.ap, out.ap, pattern, cmp, fill, base=base, channel_multiplier=cm), [out], [out])

        def sub3(view_buf, off, a, b):
            v = view_buf[:, off:off + a * b]
            return View(v.ap.rearrange("p (a b) -> p a b", b=b), v.arena, v.lo, v.hi)

        memset("pool", identf.all(), 1.0)
        asel(identf.all(), [[-1, 128]], ALU.is_equal, 0.0, 0, 1)
        cp("dve", ident.all(), identf.all())
        memset("pool", onesf.all(), 1.0)
        memset("pool", onesb.all(), 1.0)
        memset("pool", trif.all(), 1.0)
        asel(trif.all(), [[1, 128]], ALU.is_ge, 0.0, 0, -1)
        cp("dve", trib.all(), trif.all())
        cp("dve", triblk.all(), trif.all())
        S.op("pool", lambda e: e.memset(triblk[0:64, 64:128].ap, 0.0), [], [triblk.all()])
        memset("pool", halfsel.all(), 1.0)
        S.op("pool", lambda e: e.memset(halfsel[:, 0, 64:128].ap, 0.0), [], [halfsel.all()])
        S.op("pool", lambda e: e.memset(halfsel[:, 1, 0:64].ap, 0.0), [], [halfsel.all()])
        memset("pool", self_f.all(), 1.0)
        asel(self_f.all(), [[-1, H], [0, 128]], ALU.is_equal, 0.0, 0, 1)
        cp("dve", sel.all(), self_f.all())
        memset("pool", cTb.all(), 0.0)
        memset("pool", epsb.all(), EPS)
        for g, w in enumerate(WINS):
            for t in range(16):
                v = float(w) / float(min(w, t + 1))
                S.op("pool", (lambda g=g, t=t, v=v: (lambda e: e.memset(fixw[:, g, t:t + 1].ap, v)))(), [], [fixw[:, g, t:t + 1]])
        for i, nm_ in enumerate(("n1", "nm", "n2")):
            src = bass.AP(T[nm_], 0, [[1, 128], [128, KC]])
            S.dma("sp", "misc", gT[:, i, :], src, allow_slow_non_contiguous=True)
        S.dma("sp", "misc", pscT.all(), bass.AP(T["psc"], 0, [[1, 128], [128, KC]]), allow_slow_non_contiguous=True)
        S.dma("sp", "misc", bfb.all(), bass.AP(T["bf"], 0, [[0, 128], [1, H]]))

        pieces = []
        for (name, parts) in blocks:
            pl = []
            for (off, wn, r0, nr, c0, wd) in parts:
                step = max(1, PCE // wd)
                k0 = 0
                while k0 < nr:
                    nk = min(step, nr - k0)
                    pl.append((off + k0 * wd, wn, r0 + k0, nk, c0, wd))
                    k0 += nk
            pieces.append(pl)
        used_of = []
        for (name, parts) in blocks:
            used_of.append(max(off + nr * wd for (off, wn, r0, nr, c0, wd) in parts))
        cstate = {"blk": 0, "unit": 0}

        def convert_block(bi, wslot):
            for (off, wn, r0, nk, c0, wd) in pieces[bi]:
                u = cstate["unit"]
                sl = u % NPS
                ne = nk * wd
                src = A[wn][r0 * 128:(r0 + nk) * 128, c0:c0 + wd].rearrange("(k p) w -> p k w", p=128)
                dst = sub3(pst[sl], 0, nk, wd)
                S.dma("sp", "pl%d" % sl, dst, src)
                cp("act" if u % 2 == 0 else "dve", wslot[:, off:off + ne], pst[sl][:, 0:ne])
                S.dma("pool", "ps%d" % sl, wsc[bi, :, off:off + ne], wslot[:, off:off + ne],
                      writes=[View(None, "wsc", bi * SLOT + off, bi * SLOT + off + ne)])
                cstate["unit"] += 1

        CLOOK = 2
        wstate = {"issued": 0, "next": 0, "done": 0}
        NPASS = cfg.nseq_p * (cfg.S // 512) + 1
        TOTAL = NPASS * (NB + 1)

        NBIG = 8
        bigsl = [S.alloc("sb", [128, SLOT], BF16, at=(KT.off + i * 8192) if i < 4 else (Vc.off + (i - 4) * 8192)) for i in range(NBIG)]
        slot_last = {}
        slot_map = {}
        ring_ctr = {"s": 0, "b": 0}
        MAXLA = 8

        def issue_to(g):
            while wstate["issued"] <= min(g, TOTAL - 1):
                gi = wstate["issued"]
                bi = gi % (NB + 1)
                big = (gi // (NB + 1) == NPASS - 1) and bi < NB and blocks[bi][0][0] in ("GU", "D") and getattr(cfg, "bigring", True)
                if big:
                    skey = ("b", ring_ctr["b"] % NBIG)
                    buf = bigsl[skey[1]]
                    dkey = "wb%d" % skey[1]
                else:
                    skey = ("s", ring_ctr["s"] % NSLOT)
                    buf = wsl[skey[1]]
                    dkey = "w%d" % skey[1]
                if slot_last.get(skey, -1) >= wstate["done"]:
                    break
                if bi == NB:
                    v = buf.all()
                    dstv = View(v.ap.bitcast(F32), v.arena, v.lo, v.hi)
                    S.dma("sp", dkey, dstv, bass.AP(T["nf"], 0, [[0, 128], [1, D]]))
                else:
                    u = used_of[bi]
                    if gi < NB:
                        convert_block(bi, buf)
                    else:
                        S.dma("sp", dkey, buf[:, 0:u], wsc[bi, :, 0:u], reads=[View(None, "wsc", bi * SLOT, bi * SLOT + u)])
                ring_ctr["b" if big else "s"] += 1
                slot_last[skey] = gi
                slot_map[gi] = buf
                wstate["issued"] += 1

        def wget(name):
            gi = wstate["next"]
            bi = gi % (NB + 1)
            if name == "GF":
                assert bi == NB, (name, bi)
            else:
                assert blocks[bi][0] == name, (name, blocks[bi][0])
            issue_to(wstate["done"] + MAXLA - 1)
            assert gi < wstate["issued"], "weight block not issued (too many live blocks?)"
            wstate["next"] += 1
            return slot_map.pop(gi)

        def wdone(n=1):
            wstate["done"] += n
            assert wstate["done"] <= wstate["next"]
            issue_to(wstate["done"] + MAXLA - 1)

        def sq_block(s_, n_):
            act(junks[nxt("junk", 4)].all(), xt[:, s_, n_ * 512:(n_ + 1) * 512], AF.Square, accum=ssqp[:, s_, n_:n_ + 1])

        def rstd_from_parts(NT):
            S.op("dve", lambda e: e.tensor_reduce(ssq[:, 0:NT].ap, ssqp[:, 0:NT, :].ap, mybir.AxisListType.X, ALU.add),
                 [ssqp[:, 0:NT, :]], [ssq[:, 0:NT]])
            act(rstd[:, 0:NT], ssq[:, 0:NT], AF.Ln, bias=epsb.all(), scale=1.0 / D)
            act(rstd[:, 0:NT], rstd[:, 0:NT], AF.Exp, scale=-0.5)

        def norm_T(NT, gi):
            N = NT * 128
            rstd_from_parts(NT)
            for s in range(NT):
                if s < 2:
                    ts("dve", xn[:, s, :], xt[:, s, :], rstd[:, s:s + 1], ALU.mult)
                else:
                    S.op("act", (lambda s=s: (lambda e: e.mul(xn[:, s, :].ap, xt[:, s, :].ap, rstd[:, s:s + 1].ap)))(),
                         [xt[:, s, :], rstd[:, s:s + 1]], [xn[:, s, :]])
            for c in range(KC):
                b = nb_()
                for s in range(NT):
                    tr(PSB[b][:, s * 128:(s + 1) * 128], xn[:, s, c * 128:(c + 1) * 128], ident.all())
                if c % 2 == 0:
                    ts("dve", hT[:, c, 0:N], PSB[b][:, 0:N], gT[:, gi, c:c + 1], ALU.mult)
                else:
                    S.op("act", (lambda c=c, b=b: (lambda e: e.mul(hT[:, c, 0:N].ap, PSB[b][:, 0:N].ap, gT[:, gi, c:c + 1].ap)))(),
                         [PSB[b][:, 0:N], gT[:, gi, c:c + 1]], [hT[:, c, 0:N]])

        def ffn(NT, f, gi):
            N = NT * 128
            norm_T(NT, gi)
            for half in range(2):
                for jj in range(cfg.NJH):
                    j = half * cfg.NJH + jj
                    w = wget(("GU", f, j))
                    wg = sub3(w, 0, 16, 128)
                    wu = sub3(w, 2048, 16, 128)
                    bA, bB = nb_(2)
                    for kc in range(KC):
                        mm(PS[bA][:, 0:N], View(wg.ap[:, kc, :], wg.arena, wg.lo, wg.hi), hT[:, kc, 0:N], kc == 0, kc == KC - 1)
                    for kc in range(KC):
                        mm(PS[bB][:, 0:N], View(wu.ap[:, kc, :], wu.arena, wu.lo, wu.hi), hT[:, kc, 0:N], kc == 0, kc == KC - 1)
                    si = nxt("sg", 2)
                    act(sg[si][:, 0:N], PS[bA][:, 0:N], AF.Silu)
                    tt("dve", hid[:, jj, 0:N], PS[bB][:, 0:N], sg[si][:, 0:N], ALU.mult)
                    wdone()
                for n in range(4):
                    acc = nb_(4)
                    for (a, b) in jgroups(cfg.NJH):
                        w = wget(("D", f, half, n, a))
                        wv = sub3(w, 0, b - a, 512)
                        for jl in range(b - a):
                            jj = a + jl
                            for s in range(NT):
                                mm(PS[acc[s]].all(), hid[:, jj, s * 128:(s + 1) * 128],
                                   View(wv.ap[:, jl, :], wv.arena, wv.lo, wv.hi), jj == 0, jj == cfg.NJH - 1)
                        wdone()
                    for s in range(NT):
                        stt("dve", xt[:, s, n * 512:(n + 1) * 512], PS[acc[s]].all(), 0.5,
                            xt[:, s, n * 512:(n + 1) * 512], ALU.mult, ALU.add)
                        if half == 1:
                            sq_block(s, n)

        def proj_feat(NT, wname, evac):
            N = NT * 128
            w = wget(wname)
            for hh in range(2):
                wv = sub3(w, hh * 2048, 16, 128)
                b = nb_()
                for kc in range(KC):
                    mm(PS[b][:, 0:N], View(wv.ap[:, kc, :], wv.arena, wv.lo, wv.hi), hT[:, kc, 0:N], kc == 0, kc == KC - 1)
                evac(hh, b)
            wdone()

        def proj_tok(NT, wn, n, evac):
            acc = nb_(4)
            for half in range(2):
                w = wget((wn, n, half))
                wv = sub3(w, 0, 8, 512)
                for kcl in range(8):
                    kc = half * 8 + kcl
                    for s in range(NT):
                        mm(PS[acc[s]].all(), hT[:, kc, s * 128:(s + 1) * 128], View(wv.ap[:, kcl, :], wv.arena, wv.lo, wv.hi),
                           kc == 0, kc == KC - 1)
                wdone()
            for s in range(NT):
                evac(s, acc[s])

        def out_rows(dst_t, rows, col0, width, src_view):
            pass

        def mixer(tile):
            kind = tile["kind"]
            NT = tile["NT"]
            N = NT * 128
            norm_T(NT, 1)
            inv = 1.0 / math.sqrt(DH)
            for hp in range(4):
                def ev(hh, b, hp=hp):
                    S.op("act", lambda e: e.mul(qT[:, 2 * hp + hh, 0:N].ap, PS[b][:, 0:N].ap, inv), [PS[b][:, 0:N]], [qT[:, 2 * hp + hh, 0:N]])
                proj_feat(NT, ("Q", hp), ev)
            chk("mix_q")
            def rows_of(s):
                if kind == "P":
                    return [(tile["seq"], tile["pos0"] + s * 128, 0, 128)]
                return [(q, 0, q * cfg.TD, cfg.TD) for q in range(cfg.nseq_s)]

            def store(dname, s, col0, width, ti):
                if getattr(cfg, "nostore", False):
                    return
                for (sq, t0, p0, npp) in rows_of(s):
                    S.dma("pool", "o%d" % ti, A[dname][sq, t0:t0 + npp, col0:col0 + width], tmp[ti][p0:p0 + npp, 0:width])

            for n in range(2):
                def ev(s, b, n=n):
                    kd = getattr(cfg, "kdbg", 0)
                    if kd == 1:
                        return
                    ti = ntmp()
                    if kd != 3:
                        cp("act", tmp[ti].all(), PS[b].all())
                    store("kp" if kind == "P" else "ks", s, n * 512, 512, ti)
                    if kd != 2:
                        cp("dve", ktok[:, s, n * 512:(n + 1) * 512], tmp[ti].all())
                proj_tok(NT, "K", n, ev)
            chk("mix_k")
            for h in range(H):
                b = nb_()
                for s in range(NT):
                    tr(PSB[b][:, s * 128:(s + 1) * 128], ktok[:, s, h * 128:(h + 1) * 128], ident.all())
                if kind == "P":
                    cp("act" if h % 2 == 0 else "dve", KT[:, h, tile["pos0"]:tile["pos0"] + N], PSB[b][:, 0:N])
                else:
                    cp("act" if h % 2 == 0 else "dve", ktn[:, h, 0:N], PSB[b][:, 0:N])
            for n in range(2):
                def ev(s, b, n=n):
                    ti = ntmp()
                    cp("act", tmp[ti].all(), PS[b].all())
                    store("vp" if kind == "P" else "vs", s, n * 512, 512, ti)
                    if kind == "P":
                        cp("dve", Vc[:, tile["pos0"] // 128 + s, n * 512:(n + 1) * 512], tmp[ti].all())
                    else:
                        cp("dve", vnew[:, n * 512:(n + 1) * 512], tmp[ti].all())
                proj_tok(NT, "V", n, ev)
            chk("mix_v")
            w = wget(("F",))
            wv = sub3(w, 0, 16, 8)
            pts = []
            for s in range(NT):
                b = nb_()
                for kc in range(KC):
                    mm(PS[b][:, 0:H], hT[:, kc, s * 128:(s + 1) * 128], View(wv.ap[:, kc, :], wv.arena, wv.lo, wv.hi), kc == 0, kc == KC - 1)
                pt_ = (tile["pos0"] // 128 + s) if kind == "P" else NKT
                pts.append(pt_)
                tt("dve", fl[:, s, :], PS[b][:, 0:H], bfb.all(), ALU.add)
            act(fl[:, 0:NT, :], fl[:, 0:NT, :], AF.Sigmoid)
            for s in range(NT):
                pt_ = pts[s]
                act(logfc[:, pt_, :], fl[:, s, :], AF.Ln)
                for (sq, t0, p0, npp) in rows_of(s):
                    S.dma("pool", "lf", A["lp" if kind == "P" else "ls"][sq, t0:t0 + npp, :], logfc[p0:p0 + npp, pt_, :])

            def f_part1(s):
                pt_ = pts[s]
                b2 = nb_()
                if kind == "P":
                    mm(PS[b2][:, 0:H], trif.all(), logfc[:, pt_, :], True, pt_ == 0)
                    if pt_ > 0:
                        mm(PS[b2][:, 0:H], onesf.all(), ltot.all(), False, True)
                    if pt_ == 0:
                        cp("pool", ltot.all(), logfc[:, pt_, :])
                    else:
                        tt("pool", ltot.all(), ltot.all(), logfc[:, pt_, :], ALU.add)
                else:
                    mm(PS[b2][:, 0:H], triblk.all(), logfc[:, pt_, :], True, False)
                    for q in range(cfg.nseq_s):
                        mm(PS[b2][:, 0:H], halfsel[:, q, :], lptot[:, q, :], False, q == cfg.nseq_s - 1)
                cp("act", ccol.all(), PS[b2][:, 0:H])
                ts("pool", negc[:, pt_, :], ccol.all(), -1.0, ALU.mult)

            def f_part2(s):
                b3 = nb_()
                tr(PS[b3][0:H, 0:128], ccol.all(), identf.all())
                c0 = (tile["pos0"] + s * 128) if kind == "P" else KTW
                cp("act", cTb[0:H, c0:c0 + 128], PS[b3][0:H, 0:128])
            wdone()
            chk("mix_f")
            if kind == "P":
                segs = [(0, 0, N)]
            else:
                segs = [(q * (HIST + cfg.TD), q * cfg.TD, cfg.TD) for q in range(cfg.nseq_s)]
            UW = segs[-1][0] + HIST + segs[-1][2]
            for up in range(4):
                def ev(uu, b, up=up):
                    cu = 2 * up + uu
                    g = cu // 2
                    wdw = WINS[g]
                    u_ = ub[0]
                    for qi, (hc, tc, ntk) in enumerate(segs):
                        if kind == "P":
                            cp("pool", u_[:, hc:hc + HIST], hist[:, cu, :])
                        else:
                            cp("pool", u_[:, hc:hc + HIST], histS[qi][:, cu, :])
                        cp("act", u_[:, hc + HIST:hc + HIST + ntk], PS[b][:, tc:tc + ntk])
                    for qi, (hc, tc, ntk) in enumerate(segs):
                        if kind == "P":
                            cp("pool", hist[:, cu, :], u_[:, hc + ntk:hc + ntk + HIST])
                        else:
                            cp("pool", histS[qi][:, cu, :], u_[:, hc + ntk:hc + ntk + HIST])
                    cur = u_
                    sh = 1
                    k = 0
                    while sh < wdw:
                        nxtb = pa[k % 2]
                        tt("dve", nxtb[:, sh:UW], cur[:, sh:UW], cur[:, 0:UW - sh], ALU.add)
                        cur = nxtb
                        sh *= 2
                        k += 1
                    for (hc, tc, ntk) in segs:
                        if kind == "P" and tile["pos0"] == 0:
                            tt("dve", cur[:, HIST:HIST + 16], cur[:, HIST:HIST + 16], fixw[:, g, :], ALU.mult)
                        stt("dve", dT[:, cu, tc:tc + ntk], cur[:, hc + HIST:hc + HIST + ntk], 1.0 / wdw,
                            u_[:, hc + HIST:hc + HIST + ntk], ALU.mult, ALU.subtract)
                if up < NT:
                    f_part1(up)
                proj_feat(NT, ("U", up), ev)
                if up < NT:
                    f_part2(up)
            def pool_state_out():
                if kind == "S" or tile["last"]:
                    for qi in range(cfg.nseq_s if kind == "S" else 1):
                        hsrc = histS[qi] if kind == "S" else hist
                        b1, b2 = nb_(2)
                        for cu in range(8):
                            bb = b1 if cu < 4 else b2
                            tr(PS[bb][0:HIST, (cu % 4) * 128:(cu % 4 + 1) * 128], hsrc[:, cu, :], identf.all())
                        for hi_, bb in enumerate((b1, b2)):
                            ti = ntmp()
                            cp("dve", tmp[ti][0:HIST, :], PS[bb][0:HIST, :])
                            dst = A["pso"][qi] if kind == "S" else A["pp"][tile["seq"]]
                            S.dma("pool", "o%d" % ti, dst[:, hi_ * 512:(hi_ + 1) * 512], tmp[ti][0:HIST, :])

            chk("mix_pso")
            if kind == "P":
                attention_prompt(tile)
            else:
                for q in range(cfg.nseq_s):
                    attention_sample(q)
            pool_state_out()
            chk("mix_attn")
            if getattr(cfg, "debug", False) and kind == "P" and tile["pos0"] == 0 and tile["seq"] == 0:
                S.dma("pool", "lf", A["dbg_qT"], qT.all())
                S.dma("pool", "lf", A["dbg_oT"], oT.all())
                S.dma("pool", "lf", A["dbg_dT"], dT.all())
                S.dma("pool", "lf", A["dbg_negc"], negc[:, 0:4, :])
                S.dma("pool", "lf", A["dbg_cTb"], cTb[0:8, 0:512])
                S.dma("pool", "lf", A["dbg_KT"].rearrange("p (h t) -> p h t", h=8), KT[:, :, 0:512])
                S.dma("pool", "lf", A["dbg_V"], Vc[:, 0:4, :])
            for c in range(16):
                g = c // 4
                wab = wget(("AB", c))
                wa = sub3(wab, 0, 8, 128)
                wpp = sub3(wab, 1024, 2, 128)
                wgt = wget(("G", c))
                wga = sub3(wgt, 0, 16, 128)
                wgb = sub3(wgt, 2048, 16, 128)
                bA, bB, bGa, bGb = nb_(4)
                for kc in range(8):
                    mm(PS[bA][:, 0:N], View(wa.ap[:, kc, :], wa.arena, wa.lo, wa.hi), oT[:, kc, 0:N], kc == 0, kc == 7)
                for cc in range(2):
                    mm(PS[bB][:, 0:N], View(wpp.ap[:, cc, :], wpp.arena, wpp.lo, wpp.hi), dT[:, 2 * g + cc, 0:N], cc == 0, cc == 1)
                for kc in range(KC):
                    mm(PS[bGa][:, 0:N], View(wga.ap[:, kc, :], wga.arena, wga.lo, wga.hi), hT[:, kc, 0:N], kc == 0, kc == KC - 1)
                for kc in range(KC):
                    mm(PS[bGb][:, 0:N], View(wgb.ap[:, kc, :], wgb.arena, wgb.lo, wgb.hi), hT[:, kc, 0:N], kc == 0, kc == KC - 1)
                wdone(2)
                t1, t2 = ntmp(), ntmp()
                act(tmp[t1][:, 0:N], PS[bGa][:, 0:N], AF.Sigmoid)
                act(tmp[t2][:, 0:N], PS[bGb][:, 0:N], AF.Sigmoid)
                tt("dve", tmp[t1][:, 0:N], PS[bA][:, 0:N], tmp[t1][:, 0:N], ALU.mult)
                stt("dve", tmp[t2][:, 0:N], PS[bB][:, 0:N], pscT[:, c:c + 1], tmp[t2][:, 0:N], ALU.mult, ALU.mult)
                tt("dve", mT[:, c, 0:N], tmp[t1][:, 0:N], tmp[t2][:, 0:N], ALU.add)
            if getattr(cfg, "debug", False) and kind == "P" and tile["pos0"] == 0 and tile["seq"] == 0:
                S.dma("pool", "lf", A["dbg_mT"], mT.all())
            for n in range(4):
                def ev(s, b, n=n):
                    tt("dve", xt[:, s, n * 512:(n + 1) * 512], PS[b].all(), xt[:, s, n * 512:(n + 1) * 512], ALU.add)
                    sq_block(s, n)
                acc = nb_(4)
                for half in range(2):
                    w = wget(("WO", n, half))
                    wv = sub3(w, 0, 8, 512)
                    for kcl in range(8):
                        kc = half * 8 + kcl
                        for s in range(NT):
                            mm(PS[acc[s]].all(), mT[:, kc, s * 128:(s + 1) * 128], View(wv.ap[:, kcl, :], wv.arena, wv.lo, wv.hi),
                               kc == 0, kc == KC - 1)
                    wdone()
                for s in range(NT):
                    ev(s, acc[s])

        def attn_core(h, chunks, qcol0, nq, out_col0):
            pass

        SKEW = 2

        def attention_prompt(tile):
            N = 512
            i4 = tile["pos0"] // 128
            nch = i4 + 4
            pos0 = tile["pos0"]
            prevb = ()
            for h in range(H):
                bO, bD = nb_(2)
                while bO in prevb or bD in prevb:
                    bO, bD = nb_(2)
                pbuf = {}
                excl_early = (bO, bD) + tuple(prevb)
                prevb = (bO, bD)

                def st1(j, h=h, bO=bO, bD=bD, pbuf=pbuf, excl_early=excl_early):
                    q0 = max(0, j - i4) * 128
                    bS = nb_excl(excl_early if j < 3 else (bO, bD))
                    mm(PS[bS][:, q0:N], KT[:, h, j * 128:(j + 1) * 128], qT[:, h, q0:N], True, False)
                    mm(PS[bS][:, q0:N], sel[:, h, :], cTb[:, pos0 + q0:pos0 + N], False, True)
                    p_ = PT[nxt("pt", NPT)]
                    act(p_[:, q0:N], PS[bS][:, q0:N], AF.Exp, bias=negc[:, j, h:h + 1])
                    if j >= i4:
                        tt("dve", p_[:, q0:q0 + 128], p_[:, q0:q0 + 128], trib.all(), ALU.mult)
                    pbuf[j] = (p_, q0)

                def st2(j, h=h, bO=bO, bD=bD, pbuf=pbuf):
                    p_, q0 = pbuf.pop(j)
                    mm(PS[bO][:, q0:N], Vc[:, j, h * 128:(h + 1) * 128], p_[:, q0:N], j == 0, j == nch - 1)
                    mm(PS[bD][:, q0:N], onesb.all(), p_[:, q0:N], j == 0, j == nch - 1)

                for j in range(nch + SKEW):
                    if j < nch:
                        st1(j)
                    if j - SKEW >= 0:
                        st2(j - SKEW)
                ti = ntmp()
                act(tmp[ti].all(), PS[bD].all(), AF.Ln)
                act(tmp[ti].all(), tmp[ti].all(), AF.Exp, scale=-1.0)
                tt("dve", oT[:, h, 0:N], PS[bO].all(), tmp[ti].all(), ALU.mult)

        histS = [S.alloc("sb", [128, 8, HIST], F32, at=21504 + 512 * i) for i in range(cfg.nseq_s)]

        def sample_prep():
            npast = cfg.PAST // 128
            for q in range(cfg.nseq_s):
                S.dma("sp", "c0", hst[0:HIST, :], A["spool"][q])
                for cu in range(8):
                    b = nb_()
                    tr(PS[b][:, 0:HIST], hst[0:HIST, cu * 128:(cu + 1) * 128], identf[0:HIST, 0:HIST])
                    cp("dve", histS[q][:, cu, :], PS[b][:, 0:HIST])
                S.dma("sp", "c1", lpast[:, q, 0:npast, :], A["cl"][q].rearrange("(j p) h -> p j h", p=128))
                S.op("dve", (lambda q=q: (lambda e: e.tensor_reduce(lptot[:, q, :].ap, lpast[:, q, 0:npast, :].ap.rearrange("p j h -> p h j"),
                                                                   mybir.AxisListType.X, ALU.add)))(),
                     [lpast[:, q, 0:npast, :]], [lptot[:, q, :]])

        def attention_sample(q):
            TD = cfg.TD
            npast = cfg.PAST // 128
            for jt in range(npast):
                b = nb_()
                mm(PS[b][:, 0:H], trif.all(), lpast[:, q, jt, :], True, jt == 0)
                if jt > 0:
                    mm(PS[b][:, 0:H], onesf.all(), lrun.all(), False, True)
                ts("dve", negc[:, jt, :], PS[b][:, 0:H], -1.0, ALU.mult)
                if jt == 0:
                    cp("dve", lrun.all(), lpast[:, q, jt, :])
                elif jt < npast - 1:
                    tt("dve", lrun.all(), lrun.all(), lpast[:, q, jt, :], ALU.add)
            S.dma("sp", "sh0", negS[0:TD, :], negc[q * TD:(q + 1) * TD, NKT, :])
            S.dma("sp", "sh1", Vc[0:TD, NKT, :], vnew[q * TD:(q + 1) * TD, :])
            for jt in range(npast):
                ks_ = cst[nxt("cst", 2)]
                S.dma("sp", "c%d" % rot["cst"], ks_.all(), A["ck"][q, jt * 128:(jt + 1) * 128, :])
                kb = cbf[nxt("cbf", 2)]
                cp("act", kb.all(), ks_.all())
                for hq in range(2):
                    b = nb_()
                    for hh in range(4):
                        h = hq * 4 + hh
                        tr(PSB[b][:, hh * 128:(hh + 1) * 128], kb[:, h * 128:(h + 1) * 128], ident.all())
                    src = View(PSB[b][:, 0:512].ap.rearrange("p (a b) -> p a b", b=128), PSB[b].arena, 0, 1024)
                    cp("dve" if hq == 0 else "act", KT[:, hq * 4:(hq + 1) * 4, jt * 128:(jt + 1) * 128], src)
                vs_ = cst[nxt("cst", 2)]
                S.dma("sp", "c%d" % rot["cst"], vs_.all(), A["cv"][q, jt * 128:(jt + 1) * 128, :])
                cp("dve" if jt % 2 == 0 else "act", Vc[:, jt, :], vs_.all())
            qc = q * TD
            for h in range(H):
                bO, bD = nb_(2)
                for j in range(npast + 1):
                    bS = nb_excl((bO, bD))
                    p_ = PT[nxt("pt", NPT)]
                    if j < npast:
                        nk = 128
                        mm(PS[bS][:, 0:TD], KT[:, h, j * 128:(j + 1) * 128], qT[:, h, qc:qc + TD], True, False)
                        mm(PS[bS][:, 0:TD], sel[:, h, :], cTb[:, KTW + qc:KTW + qc + TD], False, True)
                        act(p_[:, 0:TD], PS[bS][:, 0:TD], AF.Exp, bias=negc[:, j, h:h + 1])
                        mm(PS[bO][:, 0:TD], Vc[:, j, h * 128:(h + 1) * 128], p_[:, 0:TD], j == 0, False)
                        mm(PS[bD][:, 0:TD], onesb.all(), p_[:, 0:TD], j == 0, False)
                    else:
                        mm(PS[bS][0:TD, 0:TD], ktn[:, h, qc:qc + TD], qT[:, h, qc:qc + TD], True, False)
                        mm(PS[bS][0:TD, 0:TD], sel[:, h, 0:TD], cTb[:, KTW + qc:KTW + qc + TD], False, True)
                        act(p_[0:TD, 0:TD], PS[bS][0:TD, 0:TD], AF.Exp, bias=negS[0:TD, h:h + 1])
                        tt("dve", p_[0:TD, 0:TD], p_[0:TD, 0:TD], trib[0:TD, 0:TD], ALU.mult)
                        mm(PS[bO][:, 0:TD], Vc[0:TD, NKT, h * 128:(h + 1) * 128], p_[0:TD, 0:TD], False, True)
                        mm(PS[bD][:, 0:TD], onesb[0:TD, :], p_[0:TD, 0:TD], False, True)
                ti = ntmp()
                act(tmp[ti][:, 0:TD], PS[bD][:, 0:TD], AF.Ln)
                act(tmp[ti][:, 0:TD], tmp[ti][:, 0:TD], AF.Exp, scale=-1.0)
                tt("dve", oT[:, h, qc:qc + TD], PS[bO][:, 0:TD], tmp[ti][:, 0:TD], ALU.mult)

        negS = S.alloc("sb", [128, H], F32, at=22528)
        lrun = S.alloc("sb", [128, H], F32, at=22528 + 64)
        print("sbuf used (final)", S.cursor("sb"), "of", ARENA)

        ystage = [S.alloc("sb", [128, 512], F32, at=hT.off + i * 2048) for i in range(8)]

        def load_x(tile, s):
            if tile["kind"] == "P":
                S.dma("sp", "x%d" % s, xt[:, s, :], A["xp"][tile["seq"], tile["pos0"] + s * 128:tile["pos0"] + (s + 1) * 128, :])
            elif s == 0:
                for q in range(cfg.nseq_s):
                    S.dma("sp", "x%d" % q, xt[q * cfg.TD:(q + 1) * cfg.TD, 0, :], A["xs"][q])

        def final_out(tile, nxt_tile=None):
            NT = tile["NT"]
            kind = tile["kind"]
            rstd_from_parts(NT)
            w = wget("GF")
            v = w.all()
            gf = View(v.ap.bitcast(F32), v.arena, v.lo, v.hi)
            for s in range(NT):
                for n in range(4):
                    gfn = View(gf.ap[:, n * 512:(n + 1) * 512], gf.arena, gf.lo, gf.hi)
                    if n % 2 == 1:
                        yi = (s * 2 + n // 2) % 8
                        stg = ystage[yi]
                        key = "y%d" % yi
                    else:
                        ti = ntmp()
                        stg = tmp[ti]
                        key = "o%d" % ti
                    stt("dve", stg.all(), xt[:, s, n * 512:(n + 1) * 512], rstd[:, s:s + 1], gfn, ALU.mult, ALU.mult)
                    if kind == "P":
                        S.dma("pool", key, A["yp"][tile["seq"], tile["pos0"] + s * 128:tile["pos0"] + (s + 1) * 128, n * 512:(n + 1) * 512], stg.all())
                    else:
                        for q in range(cfg.nseq_s):
                            S.dma("pool", key, A["ys"][q, :, n * 512:(n + 1) * 512], stg[q * cfg.TD:(q + 1) * cfg.TD, :])
                if nxt_tile is not None and s < nxt_tile["NT"]:
                    load_x(nxt_tile, s)
            wdone()

        def run_tile(tile, nxt_tile=None, first=False):
            NT = tile["NT"]
            kind = tile["kind"]
            if kind == "P" and tile["pos0"] == 0:
                memset("pool", hist.all(), 0.0)
            if first:
                for s in range(NT):
                    load_x(tile, s)
            if kind == "S":
                sample_prep()
            for s in range(NT):
                for n in range(4):
                    if s % 2 == 1 and getattr(cfg, "dve_sq", True):
                        blk = xt[:, s, n * 512:(n + 1) * 512]
                        jb = junks[nxt("junk", 4)].all()
                        S.op("dve", (lambda blk=blk, jb=jb, s=s, n=n: (lambda e: e.scalar_tensor_tensor(
                            jb.ap, blk.ap, 1.0, blk.ap, ALU.mult, ALU.mult, accum_out=ssqp[:, s, n:n + 1].ap)))(),
                            [blk], [jb, ssqp[:, s, n:n + 1]])
                    else:
                        sq_block(s, n)
            chk("xload")
            ffn(NT, 1, 0)
            chk("ffn1")
            mixer(tile)
            chk("mixer")
            ffn(NT, 2, 2)
            chk("ffn2")
            final_out(tile, nxt_tile)
            chk("tile0")
            if kind == "P" and tile["last"] and tile["seq"] == cfg.nseq_p - 1:
                chk("ptiles")

        tiles = []
        for b in range(cfg.nseq_p):
            for i in range(cfg.S // 512):
                tiles.append({"kind": "P", "NT": 4, "seq": b, "pos0": i * 512, "last": i == cfg.S // 512 - 1})
        tiles.append({"kind": "S", "NT": 1})
        try:
            if stop_at not in ("consts",):
                for it_, t in enumerate(tiles):
                    run_tile(t, tiles[it_ + 1] if it_ + 1 < len(tiles) else None, first=(it_ == 0))
            if stop_at is None:
                assert wstate["next"] == TOTAL and wstate["done"] == TOTAL, (wstate, TOTAL)
        except StopBuild as e_:
            print("STOPPED AT", e_)

        S.prepare()
        okeys = [k for k in dkeys if k.startswith("o") or k in ("lf",) or k.startswith("ps") or k.startswith("y")]
        stats = {}

        @block.tensor
        def _(e):
            stats["pe"] = S.emit_engine("pe", e, sem_e, sem_d)

        @block.scalar
        def _(e):
            stats["act"] = S.emit_engine("act", e, sem_e, sem_d)

        @block.vector
        def _(e):
            stats["dve"] = S.emit_engine("dve", e, sem_e, sem_d)

        @block.gpsimd
        def _(e):
            stats["pool"] = S.emit_engine("pool", e, sem_e, sem_d, final_keys=okeys)

        @block.sync
        def _(e):
            stats["sp"] = S.emit_engine("sp", e, sem_e, sem_d, final_keys=[k for k in dkeys if k not in okeys])
        print("ops/waits", stats)
    return nc


def make_in_maps(cfg, inp, ncores):
    maps = []
    g = lambda k: np.ascontiguousarray(np.asarray(inp[k], dtype=np.float32))
    w = {}
    w["n1"] = g("ffn1_norm")[0]
    w["wg1"] = g("ffn1_w_gate")[0]
    w["wu1"] = g("ffn1_w_up")[0]
    w["wd1"] = g("ffn1_w_down")[0]
    w["n2"] = g("ffn2_norm")[0]
    w["wg2"] = g("ffn2_w_gate")[0]
    w["wu2"] = g("ffn2_w_up")[0]
    w["wd2"] = g("ffn2_w_down")[0]
    w["nm"] = g("mix_norm")[0]
    w["win"] = g("w_in")[0]
    w["bf"] = g("b_forget")[0]
    w["wba"] = g("w_branch_attn")[0]
    wp = g("w_pool_group")[0]
    for gi in range(4):
        w["wp%d" % gi] = np.ascontiguousarray(wp[gi])
    w["psc"] = g("pool_scale")[0]
    w["wo"] = g("w_out")[0]
    w["nf"] = g("final_norm")
    xp = g("x_prompt")
    xs = g("x_sample")
    ck = g("cache_k")[0].reshape(xs.shape[0], -1, AW)
    cv = g("cache_v")[0].reshape(xs.shape[0], -1, AW)
    cl = g("cache_logf")[0]
    sp = g("state_pool")[0]
    for c in range(ncores):
        m = dict(w)
        m["xp"] = np.ascontiguousarray(xp[c * cfg.nseq_p:(c + 1) * cfg.nseq_p])
        sl = slice(c * cfg.nseq_s, (c + 1) * cfg.nseq_s)
        m["xs"] = np.ascontiguousarray(xs[sl])
        m["ck"] = np.ascontiguousarray(ck[sl])
        m["cv"] = np.ascontiguousarray(cv[sl])
        m["cl"] = np.ascontiguousarray(cl[sl])
        m["spool"] = np.ascontiguousarray(sp[sl])
        maps.append(m)
    return maps


def gather(cfg, results):
    cat = lambda k: np.concatenate([r[k] for r in results], axis=0)
    yp = cat("yp")
    ys = cat("ys")
    kp = cat("kp").reshape(1, -1, cfg.S, H, DH)
    vp = cat("vp").reshape(1, -1, cfg.S, H, DH)
    lp = cat("lp")[None]
    pp = cat("pp")[None]
    ks = cat("ks").reshape(1, -1, cfg.TD, H, DH)
    vs = cat("vs").reshape(1, -1, cfg.TD, H, DH)
    ls = cat("ls")[None]
    pso = cat("pso")[None]
    return (yp, ys, kp, vp, lp, pp, ks, vs, ls, pso)


_CACHE = {}


def kernel(**inputs):
    cfg = Cfg()
    ncores = 8
    if "nc" not in _CACHE:
        _CACHE["nc"] = build_program(cfg)
    nc = _CACHE["nc"]
    maps = make_in_maps(cfg, inputs, ncores)
    res = run_bass_kernel_spmd(nc, maps, core_ids=list(range(ncores)))
    return gather(cfg, res.results)
```
